# Optimizing a Trainium2 kernel written in Bass

```python
import jax, jax.numpy as jnp
from jax import lax
import numpy as np

D_MODEL = 1024
BATCH = 8
SEQ = 2048
DEPTH = 1
DEC_BATCH = 128
DEC_SEQ = 4
PAST_LEN = 16384
PAGE_SIZE = 128

D_A = D_MODEL
D_P = D_MODEL
POOL_WINDOWS = (2, 4, 8, 16)
N_POOL_GROUPS = len(POOL_WINDOWS)
POOL_GROUP = D_P // N_POOL_GROUPS
POOL_MAX = max(POOL_WINDOWS)
CONV_W = 3
D_FF = 2816
PLE_DIM = 256
EPS = 1e-6
D_IN_ALL = 3 * D_A + D_P + 2 * D_MODEL

kernel_name = "hybrid_shortconv_pool_convffn_step"


def rmsnorm(x, g):
    xf = x.astype(jnp.float32)
    ms = jnp.mean(xf * xf, axis=-1, keepdims=True)
    return (xf * lax.rsqrt(ms + EPS)).astype(x.dtype) * g


def causal_dwconv(v, buf, w):
    L = v.shape[1]
    ext = jnp.concatenate([buf.astype(v.dtype), v], axis=1)
    y = ext[:, 0:L] * w[0]
    for k in range(1, CONV_W):
        y = y + ext[:, k:k + L] * w[k]
    return y, ext[:, L:]


def multiscale_pool(u, buf, start_pos, w_grp, scale):
    Bsz, L, C = u.shape
    ext = jnp.concatenate([buf.astype(u.dtype), u], axis=1)
    cs = jnp.cumsum(ext.astype(jnp.float32), axis=1)
    cs = jnp.concatenate([jnp.zeros((Bsz, 1, C), jnp.float32), cs], axis=1)
    csg = cs.reshape(Bsz, POOL_MAX + L, N_POOL_GROUPS, POOL_GROUP)
    end = csg[:, POOL_MAX:]
    pos = start_pos + jnp.arange(L)
    outs = []
    for gi, w in enumerate(POOL_WINDOWS):
        s = csg[:, POOL_MAX - w:POOL_MAX - w + L, gi]
        cnt = jnp.minimum(pos + 1, w).astype(jnp.float32)
        outs.append((end[:, :, gi] - s) / cnt[None, :, None])
    pooled = jnp.stack(outs, axis=2)
    ug = u.reshape(Bsz, L, N_POOL_GROUPS, POOL_GROUP).astype(jnp.float32)
    d = (pooled - ug).astype(u.dtype)
    y = jnp.einsum('blgc,gcd->blgd', d, w_grp).reshape(Bsz, L, C) * scale
    return y, ext[:, L:]


def hybrid_layer(x, p, conv_buf, pool_buf, ffn_buf, start_pos,
                 g_pre_mix, w_in, conv_a_w, w_a_out, pool_w, pool_scale, w_o, g_post_mix,
                 g_pre_ffn, w_up, ffn_conv_w, w_down, g_post_ffn, w_ple_proj, w_ple_gate):
    xn = rmsnorm(x, g_pre_mix)
    proj = xn @ w_in
    splits = np.cumsum([D_A, D_A, D_A, D_P, D_MODEL])
    b_a, c_a, h_a, u_p, gate_a, gate_p = jnp.split(proj, splits, axis=-1)
    conv_out, new_conv = causal_dwconv(c_a * h_a, conv_buf, conv_a_w)
    y_a = (b_a * conv_out) @ w_a_out
    y_p, new_pool = multiscale_pool(u_p, pool_buf, start_pos, pool_w, pool_scale)
    merged = jax.nn.sigmoid(gate_a) * y_a + jax.nn.sigmoid(gate_p) * y_p
    x = x + rmsnorm(merged @ w_o, g_post_mix)
    hn = rmsnorm(x, g_pre_ffn)
    up = hn @ w_up
    a, g = jnp.split(up, [D_FF], axis=-1)
    a_c, new_ffn = causal_dwconv(a, ffn_buf, ffn_conv_w)
    f = (jax.nn.gelu(a_c, approximate=True) * g) @ w_down
    x = x + rmsnorm(f, g_post_ffn)
    x = x + jax.nn.sigmoid(x @ w_ple_gate) * (p @ w_ple_proj)
    return x, new_conv, new_pool, new_ffn


def setup_inputs(seed: int = 0) -> dict:
    key = jax.random.key(seed)
    ks = jax.random.split(key, 24)
    f32 = jnp.float32

    def nrm(k, shape, scale):
        return jax.random.normal(k, shape, f32) * scale

    def gain(k, shape):
        return 1.0 + 0.05 * jax.random.normal(k, shape, f32)

    return {
        "x_prompt": nrm(ks[0], (BATCH, SEQ, D_MODEL), 1.0),
        "x_sample": nrm(ks[1], (DEC_BATCH, DEC_SEQ, D_MODEL), 1.0),
        "p_prompt": nrm(ks[2], (DEPTH, BATCH, SEQ, PLE_DIM), 1.0),
        "p_sample": nrm(ks[3], (DEPTH, DEC_BATCH, DEC_SEQ, PLE_DIM), 1.0),
        "state_conv_a": nrm(ks[4], (DEPTH, DEC_BATCH, CONV_W - 1, D_A), 1.0),
        "state_pool": nrm(ks[5], (DEPTH, DEC_BATCH, POOL_MAX - 1, D_P), 1.0),
        "state_ffn_conv": nrm(ks[6], (DEPTH, DEC_BATCH, CONV_W - 1, D_FF), 1.0),
        "g_pre_mix": gain(ks[7], (DEPTH, D_MODEL)),
        "w_in": nrm(ks[8], (DEPTH, D_MODEL, D_IN_ALL), D_MODEL ** -0.5),
        "conv_a_w": nrm(ks[9], (DEPTH, CONV_W, D_A), CONV_W ** -0.5),
        "w_a_out": nrm(ks[10], (DEPTH, D_A, D_MODEL), D_A ** -0.5),
        "pool_w": nrm(ks[11], (DEPTH, N_POOL_GROUPS, POOL_GROUP, POOL_GROUP), POOL_GROUP ** -0.5),
        "pool_scale": gain(ks[12], (DEPTH, D_P)),
        "w_o": nrm(ks[13], (DEPTH, D_MODEL, D_MODEL), D_MODEL ** -0.5),
        "g_post_mix": gain(ks[14], (DEPTH, D_MODEL)),
        "g_pre_ffn": gain(ks[15], (DEPTH, D_MODEL)),
        "w_up": nrm(ks[16], (DEPTH, D_MODEL, 2 * D_FF), D_MODEL ** -0.5),
        "ffn_conv_w": nrm(ks[17], (DEPTH, CONV_W, D_FF), CONV_W ** -0.5),
        "w_down": nrm(ks[18], (DEPTH, D_FF, D_MODEL), D_FF ** -0.5),
        "g_post_ffn": gain(ks[19], (DEPTH, D_MODEL)),
        "w_ple_proj": nrm(ks[20], (DEPTH, PLE_DIM, D_MODEL), PLE_DIM ** -0.5),
        "w_ple_gate": nrm(ks[21], (DEPTH, D_MODEL, D_MODEL), D_MODEL ** -0.5),
    }


def reference(x_prompt, x_sample, p_prompt, p_sample, state_conv_a, state_pool, state_ffn_conv,
              g_pre_mix, w_in, conv_a_w, w_a_out, pool_w, pool_scale, w_o, g_post_mix,
              g_pre_ffn, w_up, ffn_conv_w, w_down, g_post_ffn, w_ple_proj, w_ple_gate):
    xp, xs = x_prompt, x_sample
    conv_p, pool_p, ffn_p = [], [], []
    conv_s, pool_s, ffn_s = [], [], []
    for i in range(DEPTH):
        w = (g_pre_mix[i], w_in[i], conv_a_w[i], w_a_out[i], pool_w[i], pool_scale[i], w_o[i],
             g_post_mix[i], g_pre_ffn[i], w_up[i], ffn_conv_w[i], w_down[i], g_post_ffn[i],
             w_ple_proj[i], w_ple_gate[i])
        zc = jnp.zeros((BATCH, CONV_W - 1, D_A), xp.dtype)
        zp = jnp.zeros((BATCH, POOL_MAX - 1, D_P), xp.dtype)
        zf = jnp.zeros((BATCH, CONV_W - 1, D_FF), xp.dtype)
        xp, c1, p1, f1 = hybrid_layer(xp, p_prompt[i], zc, zp, zf, 0, *w)
        xs, c2, p2, f2 = hybrid_layer(xs, p_sample[i], state_conv_a[i], state_pool[i],
                                      state_ffn_conv[i], PAST_LEN, *w)
        conv_p.append(c1); pool_p.append(p1); ffn_p.append(f1)
        conv_s.append(c2); pool_s.append(p2); ffn_s.append(f2)
    new_conv_a_prompt = jnp.stack(conv_p, axis=0)
    new_pool_prompt = jnp.stack(pool_p, axis=0)
    new_ffn_prompt = jnp.stack(ffn_p, axis=0)
    new_conv_a_sample = jnp.stack(conv_s, axis=0)
    new_pool_sample = jnp.stack(pool_s, axis=0)
    new_ffn_sample = jnp.stack(ffn_s, axis=0)
    return (xp, xs, new_conv_a_prompt, new_pool_prompt, new_ffn_prompt,
            new_conv_a_sample, new_pool_sample, new_ffn_sample)
```

```python
import numpy as np
import concourse.bass as bass
import concourse.mybir as mybir
from concourse.bass_utils import run_bass_kernel_spmd

F32 = mybir.dt.float32
BF16 = mybir.dt.bfloat16
I32 = mybir.dt.int32
AF = mybir.ActivationFunctionType
ALU = mybir.AluOpType

D = 1024
DFF = 2816
NFC = DFF // 128
PLE = 256
EPS = 1e-6
NCORES = 8
SEQ = 2048
DEC_B = 128
DEC_S = 4
SPC = DEC_B // NCORES
POOLW = (2, 4, 8, 16)

SLOT_ELEMS = 8192
NSLOTS = 4
NTEMPS = 12
TEMPW = 528


class Buf:
    __slots__ = ("name", "w", "r", "excl")

    def __init__(self, name, excl=False):
        self.name = name
        self.w = None
        self.r = {}
        self.excl = excl


class Tracker:
    ENG = ("pe", "act", "dve", "pool", "sp")

    def __init__(self):
        self.streams = {e: [] for e in self.ENG}
        self.count = {e: 0 for e in self.ENG}
        self.waited = {e: {} for e in self.ENG}
        self.dma_count = {}
        self.out_dma = []

    def dma_sem(self, name):
        if name not in self.dma_count:
            self.dma_count[name] = 0
        return name

    def emit(self, eng, fn, reads=(), writes=(), dma=None, is_output=False):
        need = {}

        def add(tok, kind):
            if tok is None:
                return
            key, val, teng, isdma = tok
            if not isdma and teng == eng:
                if eng == "pe":
                    return
                if kind in ("WAR", "RAR", "WAW"):
                    return
            if need.get(key, 0) < val:
                need[key] = val

        for b in reads:
            add(b.w, "RAW")
            if b.excl:
                for t in b.r.values():
                    add(t, "RAR")
        for b in writes:
            add(b.w, "WAW")
            for t in b.r.values():
                add(t, "WAR")
        waits = []
        wd = self.waited[eng]
        for key, val in need.items():
            if wd.get(key, 0) >= val:
                continue
            wd[key] = val
            waits.append((key, val))
        if dma is not None:
            self.dma_count[dma] += 16
            tok = (dma, self.dma_count[dma], eng, True)
            inc = (dma, 16)
            if is_output:
                self.out_dma.append(tok)
        else:
            self.count[eng] += 1
            tok = (eng, self.count[eng], eng, False)
            inc = (eng, 1)
        self.streams[eng].append((waits, fn, inc))
        for b in reads:
            old = b.r.get(tok[0])
            if old is None or old[1] < tok[1]:
                b.r[tok[0]] = tok
        for b in writes:
            b.w = tok
            b.r = {}
        return tok


def build_program(nt_prompt=4, with_sample=True, debug=False):
    nc = bass.Bass("TRN2", target_bir_lowering=False)
    NTOKP = nt_prompt * 512
    NS = SPC * DEC_S

    def din(name, shape):
        return nc.dram_tensor(name, list(shape), F32, kind="ExternalInput").ap()

    def dout(name, shape):
        return nc.dram_tensor(name, list(shape), F32, kind="ExternalOutput").ap()

    xp = din("xp", [NTOKP, D]); xs = din("xs", [NS, D])
    pp = din("pp", [NTOKP, PLE]); psm = din("ps", [NS, PLE])
    sca = din("sca", [SPC * 2, D]); spl = din("spl", [SPC * 15, D]); sff = din("sff", [SPC * 2, DFF])
    g_pre_mix = din("g_pre_mix", [D]); w_in = din("w_in", [D, 6 * D]); conv_a_w = din("conv_a_w", [3, D])
    w_a_out = din("w_a_out", [D, D]); pool_w = din("pool_w", [4, 256, 256]); pool_scale = din("pool_scale", [D])
    w_o = din("w_o", [D, D]); g_post_mix = din("g_post_mix", [D]); g_pre_ffn = din("g_pre_ffn", [D])
    w_up = din("w_up", [D, 2 * DFF]); ffn_conv_w = din("ffn_conv_w", [3, DFF]); w_down = din("w_down", [DFF, D])
    g_post_ffn = din("g_post_ffn", [D]); w_ple_proj = din("w_ple_proj", [PLE, D]); w_ple_gate = din("w_ple_gate", [D, D])

    yp = dout("yp", [NTOKP, D]); ys = dout("ys", [NS, D])
    ncp = dout("ncp", [2, D]); npp = dout("npp", [15, D]); nfp = dout("nfp", [2, DFF])
    ncs = dout("ncs", [SPC * 2, D]); nps = dout("nps", [SPC * 15, D]); nfs = dout("nfs", [SPC * 2, DFF])

    T = Tracker()
    from contextlib import ExitStack
    es = ExitStack()

    def sb(name, shape, dt=F32):
        return es.enter_context(nc.sbuf_tensor(name, list(shape), dt))

    xres = sb("xres", [128, 4, D]); xres_b = [Buf(f"xres{s}") for s in range(4)]
    xres_s = sb("xres_s", [128, 1, D]); xres_sb = [Buf("xres_s")]
    xb = [sb(f"xb{i}", [128, D], BF16) for i in range(2)]; xb_b = [Buf(f"xb{i}") for i in range(2)]
    actT = sb("actT", [128, 8, 512], BF16); actT_b = [Buf(f"actT_s{s}") for s in range(4)]
    big2 = sb("big", [128, 24 * 512], BF16); big_b = [[Buf(f"big{h}_{c}") for c in range(24)] for h in range(2)]
    big = big2[:, :].rearrange("p (c t) -> p c t", t=512)
    xstage = big2[:, :].bitcast(F32)

    def xst(s):
        return xstage[:, s * 1024:(s + 1) * 1024]

    def xst_b(s):
        return [big_b[h][i] for h in range(2) for i in range(4 * s, 4 * s + 4)]
    pT = sb("pT", [128, 2, 512], BF16); pT_b = [Buf(f"pT{h}") for h in range(2)]
    actT_s = sb("actT_s", [128, 8, 64], BF16); actT_sb = [Buf("actTs")]
    big_s = sb("big_s", [128, 24, 64], BF16); big_sb = [Buf(f"bigs_{c}") for c in range(24)]
    pT_s = sb("pT_s", [128, 2, 64], BF16); pT_sb = Buf("pTs")
    pf = sb("pf", [128, PLE]); pf_b = Buf("pf")
    pb = sb("pb", [128, PLE], BF16); pb_b = Buf("pb")
    temps = [sb(f"tmp{i}", [128, TEMPW]) for i in range(NTEMPS)]; temps_b = [Buf(f"tmp{i}") for i in range(NTEMPS)]
    junk = sb("junk", [128, D], BF16); junk_b = Buf("junk")
    tns = [sb(f"tn{i}", [128, D]) for i in range(2)]; tns_b = [Buf(f"tn{i}") for i in range(2)]
    tn_rr = [0]
    sg5 = [sb(f"sg5_{i}", [128, 512]) for i in range(2)]; sg5_b = [Buf(f"sg5_{i}") for i in range(2)]
    sg_rr = [0]
    stats = sb("stats", [128, 16]); stats_b = [Buf(f"st{i}") for i in range(16)]
    gbc = sb("gbc", [128, 4, D]); gbc_b = [Buf(f"gbc{i}") for i in range(4)]
    vrow = sb("vrow", [128, 128]); vrow_b = Buf("vrow")
    vecT = sb("vecT", [128, 128]); vecT_b = Buf("vecT")
    identf = sb("identf", [128, 128]); identb = sb("identb", [128, 128], BF16); ident_b = Buf("ident")
    iot = sb("iot", [128, 128], I32); iot_b = Buf("iot")
    rc16 = sb("rc16", [128, 4, 16]); rc16_b = Buf("rc16")
    io16 = sb("io16", [128, 16]); io16_b = Buf("io16")
    epsT = sb("epsT", [128, 1]); epsT_b = Buf("epsT")
    slots = [sb(f"slot{i}", [128, SLOT_ELEMS], BF16) for i in range(NSLOTS)]; slots_b = [Buf(f"slot{i}") for i in range(NSLOTS)]
    cA_p = sb("cA_p", [128, 8, 2]); cP_p = sb("cP_p", [128, 8, 15]); cF_p = sb("cF_p", [128, NFC, 2])
    cA_s = sb("cA_s", [128, 8, SPC * 2]); cP_s = sb("cP_s", [128, 8, SPC * 15]); cF_s = sb("cF_s", [128, NFC, SPC * 2])
    cA_pb = [Buf(f"cAp{c}") for c in range(8)]; cP_pb = [Buf(f"cPp{c}") for c in range(8)]; cF_pb = [Buf(f"cFp{c}") for c in range(NFC)]
    cA_sb = [Buf(f"cAs{c}") for c in range(8)]; cP_sb = [Buf(f"cPs{c}") for c in range(8)]; cF_sb = [Buf(f"cFs{c}") for c in range(NFC)]
    strow = tns[1]; strow_b = tns_b[1]

    banks = [es.enter_context(nc.psum_tensor(f"bank{i}", [128, 512], F32)) for i in range(8)]
    banks_b = [Buf(f"bank{i}", excl=True) for i in range(8)]
    bank_rr = [0]

    def take_banks(n):
        i = bank_rr[0]
        if i + n > 8:
            i = 0
        bank_rr[0] = (i + n) % 8
        return list(range(i, i + n))

    tmp_rr = [0]

    def take_tmp():
        i = tmp_rr[0]
        tmp_rr[0] = (i + 1) % NTEMPS
        return i

    st_rr = [0]

    def take_stat():
        i = st_rr[0]
        st_rr[0] = (i + 1) % 16
        return i

    xb_rr = [0]

    def ACT(fn, reads=(), writes=()):
        return T.emit("act", fn, reads, writes)

    def DVE(fn, reads=(), writes=()):
        return T.emit("dve", fn, reads, writes)

    def POOL(fn, reads=(), writes=()):
        return T.emit("pool", fn, reads, writes)

    stage = ["init"]
    pe_labels = []

    def PE(fn, reads=(), writes=()):
        pe_labels.append(stage[0])
        return T.emit("pe", fn, reads, writes)

    def DMA(eng, semname, fn, reads=(), writes=(), is_output=False):
        T.dma_sem(semname)
        return T.emit(eng, fn, reads, writes, dma=semname, is_output=is_output)

    CAW0 = 0
    PSC0 = 24
    FCW0 = 32

    DMA("sp", "c_gbc0", lambda e: e.dma_start(out=gbc[:, 0, :], in_=g_pre_mix.partition_broadcast(128)), writes=[gbc_b[0]])
    DMA("sp", "c_gbc1", lambda e: e.dma_start(out=gbc[:, 1, :], in_=g_post_mix.partition_broadcast(128)), writes=[gbc_b[1]])
    DMA("sp", "c_gbc2", lambda e: e.dma_start(out=gbc[:, 2, :], in_=g_pre_ffn.partition_broadcast(128)), writes=[gbc_b[2]])
    DMA("sp", "c_gbc3", lambda e: e.dma_start(out=gbc[:, 3, :], in_=g_post_ffn.partition_broadcast(128)), writes=[gbc_b[3]])
    POOL(lambda e: e.memset(vrow[:], 0.0), writes=[vrow_b])
    DMA("sp", "c_vrow", lambda e: e.dma_start(out=vrow[0:24, :], in_=conv_a_w.rearrange("k (c p) -> (k c) p", p=128)), writes=[vrow_b])
    DMA("sp", "c_vrow", lambda e: e.dma_start(out=vrow[24:32, :], in_=pool_scale.rearrange("(c p) -> c p", p=128)), writes=[vrow_b])
    DMA("sp", "c_vrow", lambda e: e.dma_start(out=vrow[32:98, :], in_=ffn_conv_w.rearrange("k (c p) -> (k c) p", p=128)), writes=[vrow_b])
    POOL(lambda e: e.iota(iot[:], pattern=[[1, 128]], base=0, channel_multiplier=-1), writes=[iot_b])
    DVE(lambda e: e.tensor_copy(out=identf[:], in_=iot[:]), reads=[iot_b], writes=[ident_b])
    DVE(lambda e: e.tensor_single_scalar(out=identf[:], in_=identf[:], scalar=0.0, op=ALU.is_equal), reads=[ident_b], writes=[ident_b])
    DVE(lambda e: e.tensor_copy(out=identb[:], in_=identf[:]), reads=[ident_b], writes=[ident_b])
    POOL(lambda e: e.iota(iot[:, 0:16], pattern=[[1, 16]], base=1, channel_multiplier=0), reads=[ident_b], writes=[iot_b])
    DVE(lambda e: e.tensor_copy(out=io16[:], in_=iot[:, 0:16]), reads=[iot_b], writes=[io16_b])
    for g, w in enumerate(POOLW):
        DVE(lambda e, g=g, w=w: e.tensor_scalar(out=rc16[:, g, :], in0=io16[:], scalar1=float(w), scalar2=None, op0=ALU.min),
            reads=[io16_b], writes=[rc16_b])
    DVE(lambda e: e.reciprocal(out=rc16[:], in_=rc16[:]), reads=[rc16_b], writes=[rc16_b])
    def emit_vecT():
        bk = take_banks(1)[0]
        PE(lambda e, bk=bk: e.transpose(out=banks[bk][:, 0:128], in_=vrow[:], identity=identf[:]), reads=[vrow_b, ident_b], writes=[banks_b[bk]])
        DVE(lambda e, bk=bk: e.tensor_copy(out=vecT[:], in_=banks[bk][:, 0:128]), reads=[banks_b[bk]], writes=[vecT_b])
    POOL(lambda e: e.memset(epsT[:], EPS), writes=[epsT_b])
    POOL(lambda e: e.memset(cA_p[:], 0.0), writes=cA_pb)
    POOL(lambda e: e.memset(cP_p[:], 0.0), writes=cP_pb)
    POOL(lambda e: e.memset(cF_p[:], 0.0), writes=cF_pb)

    def sview(slot, nk, ncols):
        return slot[:, 0:nk * ncols].rearrange("p (k n) -> p k n", n=ncols)

    def wsrc(w, k0, nk, c0, ncols):
        return w[k0 * 128:(k0 + nk) * 128, c0:c0 + ncols].rearrange("(k p) n -> p k n", p=128)

    tile_blocks = []
    for j in range(4):
        tile_blocks.append((f"A{j}", [(lambda sl, s=s: sview(sl, 8, 768)[:, :, s * 256:(s + 1) * 256], wsrc(w_in, 0, 8, s * D + 256 * j, 256)) for s in range(3)], 8 * 768))
        if j == 0:
            tile_blocks.append(("U", [(lambda sl: sview(sl, 8, D), wsrc(w_in, 0, 8, 3 * D, D))], 8 * D))

    def gblock(jg):
        return (f"G{jg}", [(lambda sl, s=s: sview(sl, 8, 1024)[:, :, s * 512:(s + 1) * 512], wsrc(w_in, 0, 8, (4 + s) * D + 512 * jg, 512)) for s in range(2)], 8 * 1024)
    tile_blocks.append(gblock(0))
    tile_blocks.append(("AO", [(lambda sl: sview(sl, 8, D), wsrc(w_a_out, 0, 8, 0, D))], 8 * D))
    tile_blocks.append(("PW", [(lambda sl: sview(sl, 8, 256), pool_w.rearrange("g (k p) n -> p (g k) n", p=128))], 8 * 256))
    tile_blocks.append(gblock(1))
    tile_blocks.append(("O", [(lambda sl: sview(sl, 8, D), wsrc(w_o, 0, 8, 0, D))], 8 * D))
    for j in range(6):
        ncol = 512 if j < 5 else 256
        tile_blocks.append((f"UP{j}", [(lambda sl, s=s, ncol=ncol: sview(sl, 8, 1024)[:, :, s * 512:s * 512 + ncol], wsrc(w_up, 0, 8, s * DFF + 512 * j, ncol)) for s in range(2)], 8 * 1024))
    DKB = [(0, 8), (8, 8), (16, 6)]
    for kb, (k0, nk) in enumerate(DKB):
        tile_blocks.append((f"D{kb}", [(lambda sl, nk=nk: sview(sl, nk, D), wsrc(w_down, k0, nk, 0, D))], nk * D))
    tile_blocks.append(("E1", [(lambda sl: sview(sl, 8, D), wsrc(w_ple_gate, 0, 8, 0, D))], 8 * D))
    tile_blocks.append(("E2", [(lambda sl: sview(sl, 2, D), wsrc(w_ple_proj, 0, 2, 0, D))], 2 * D))
    NBLK = len(tile_blocks)
    n_tiles_total = nt_prompt
    NG = NBLK * n_tiles_total
    wscr = nc.dram_tensor("wscratch", [NBLK, 128, SLOT_ELEMS], BF16, kind="Internal").ap()
    wscr_b = [Buf(f"wscr{i}") for i in range(NBLK)]
    ws = {"next_load": 0, "next_acq": 0, "free": list(range(NSLOTS)), "slot_of": {}}

    ALL_CAST = True

    def emit_load(j, si):
        bi = j % NBLK
        name, dmas, nel = tile_blocks[bi]
        if ALL_CAST:
            for dst_fn, src in dmas:
                DMA("pool", f"w_slot{si}", lambda e, dst_fn=dst_fn, src=src, si=si: e.dma_start(out=dst_fn(slots[si]), in_=src), writes=[slots_b[si]])
            return
        if j < NBLK:
            for dst_fn, src in dmas:
                DMA("pool", f"w_slot{si}", lambda e, dst_fn=dst_fn, src=src, si=si: e.dma_start(out=dst_fn(slots[si]), in_=src), writes=[slots_b[si]])
            if n_tiles_total > 1:
                DMA("sp", f"w_wb{bi}", lambda e, si=si, bi=bi, nel=nel: e.dma_start(out=wscr[bi, :, 0:nel], in_=slots[si][:, 0:nel]), reads=[slots_b[si]], writes=[wscr_b[bi]])
        else:
            DMA("sp", f"w_slot{si}", lambda e, si=si, bi=bi, nel=nel: e.dma_start(out=slots[si][:, 0:nel], in_=wscr[bi, :, 0:nel]), reads=[wscr_b[bi]], writes=[slots_b[si]])

    def pump_loads():
        while ws["next_load"] < NG and ws["free"]:
            si = ws["free"].pop(0)
            ws["slot_of"][ws["next_load"]] = si
            emit_load(ws["next_load"], si)
            ws["next_load"] += 1

    def acquire(name):
        j = ws["next_acq"]
        assert tile_blocks[j % NBLK][0] == name, (tile_blocks[j % NBLK][0], name)
        ws["next_acq"] += 1
        pump_loads()
        assert ws["next_load"] > j, "weight ring deadlock: block %d (%s) not loadable" % (j, name)
        return j, ws["slot_of"][j]

    def release(j):
        ws["free"].append(ws["slot_of"][j])
        pump_loads()

    class Ctx:
        pass

    def make_prompt_ctx(tidx, h=None):
        c = Ctx()
        hs = (0, 1) if h is None else (h,)
        c.kind = "prompt"; c.key = "p" + ("f" if h is None else "ab"[h])
        c.NSUB = 2 * len(hs); c.TOK = 256 * len(hs); c.L = c.TOK; c.nseq = 1; c.R = 128
        col0 = 256 * hs[0]
        sub0 = 2 * hs[0]
        r0 = tidx * 512 + col0
        c.x_src = xp[r0:r0 + c.TOK, :]; c.p_src = pp[r0:r0 + c.TOK, :]; c.y_dst = yp[r0:r0 + c.TOK, :]
        c.xres = lambda s, sub0=sub0: xres[0:128, sub0 + s, :]
        c.xres_b = xres_b[sub0:sub0 + c.NSUB]
        c.xkey = lambda s, sub0=sub0: f"p{sub0 + s}"
        cs = slice(col0, col0 + c.TOK)
        c.actT = actT[:, :, cs]; c.actT_b = actT_b[sub0:sub0 + c.NSUB]
        c.actT_sub = lambda s, sub0=sub0: [actT_b[sub0 + s]]
        c.za = big[:, 0:8, cs]; c.dd = big[:, 8:16, cs]; c.mg = big[:, 16:24, cs]; c.ff = big[:, 0:NFC, cs]
        c.bufs = lambda lo, hi: [b_ for h_ in hs for b_ in big_b[h_][lo:hi]]
        c.buf1 = lambda i: [big_b[h_][i] for h_ in hs]
        c.pT = pT[:, :, cs]; c.pT_b = [pT_b[h_] for h_ in hs]
        c.cA, c.cP, c.cF = cA_p, cP_p, cF_p
        c.cA_b, c.cP_b, c.cF_b = cA_pb, cP_pb, cF_pb
        c.first = (tidx == 0 and hs[0] == 0)
        c.g0 = (tidx == 0)
        return c

    def make_sample_ctx():
        c = Ctx()
        c.kind = "sample"; c.key = "s"
        c.L = DEC_S; c.nseq = SPC; c.R = NS; c.NSUB = 1; c.TOK = NS
        c.x_src = xs; c.p_src = psm; c.y_dst = ys
        c.xres = lambda s: xres_s[0:NS, 0, :]
        c.xres_b = xres_sb
        c.xkey = lambda s: "s"
        c.actT = actT_s[:, :, :]; c.actT_b = actT_sb
        c.actT_sub = lambda s: actT_sb
        c.za = big_s[:, 0:8, :]; c.dd = big_s[:, 8:16, :]; c.mg = big_s[:, 16:24, :]; c.ff = big_s[:, 0:NFC, :]
        c.bufs = lambda lo, hi: big_sb[lo:hi]
        c.buf1 = lambda i: [big_sb[i]]
        c.pT = pT_s[:, :, :]; c.pT_b = [pT_sb]
        c.cA, c.cP, c.cF = cA_s, cP_s, cF_s
        c.cA_b, c.cP_b, c.cF_b = cA_sb, cP_sb, cF_sb
        c.first = False
        c.g0 = True
        return c

    def PO(c):
        return DVE

    def DQ(c):
        return "pool"

    def ext_view(c, ti, pre):
        n = c.nseq * (pre + c.L)
        return temps[ti][:, 0:n].rearrange("p (s j) -> p s j", j=pre + c.L)

    def tok_view(c, ap2d):
        return ap2d.rearrange("p (s j) -> p s j", j=c.L)

    def carry_view(c, carr, ch, pre):
        return carr[:, ch, :].rearrange("p (s j) -> p s j", j=pre)

    def row_rstd(c, src_ap, src_bufs):
        si = take_stat()
        ACT(lambda e, si=si: e.memzero(stats[:, si:si + 1]), writes=[stats_b[si]])
        ACT(lambda e, si=si: e.activation(out=junk[0:c.R, :], in_=src_ap, func=AF.Square, scale=1.0 / 32.0, accum_out=stats[0:c.R, si:si + 1]),
            reads=list(src_bufs), writes=[junk_b, stats_b[si]])
        ACT(lambda e, si=si: e.activation(out=stats[0:c.R, si:si + 1], in_=stats[0:c.R, si:si + 1], func=AF.Sqrt, bias=epsT[0:c.R, 0:1], scale=1.0),
            reads=[stats_b[si], epsT_b], writes=[stats_b[si]])
        DVE(lambda e, si=si: e.reciprocal(out=stats[0:c.R, si:si + 1], in_=stats[0:c.R, si:si + 1]), reads=[stats_b[si]], writes=[stats_b[si]])
        return si

    def norm_to_bf16(c, s, gidx, src=None, src_b=None):
        src = c.xres(s) if src is None else src
        src_b = [c.xres_b[s]] if src_b is None else src_b
        si = row_rstd(c, src, src_b)
        xi = xb_rr[0]
        xb_rr[0] = (xi + 1) % 2
        DVE(lambda e, si=si, xi=xi: e.scalar_tensor_tensor(out=xb[xi][0:c.R, :], in0=src, scalar=stats[0:c.R, si:si + 1],
                                                             in1=gbc[0:c.R, gidx, :], op0=ALU.mult, op1=ALU.mult),
            reads=list(src_b) + [stats_b[si], gbc_b[gidx]], writes=[xb_b[xi]])
        return xi

    def transposes_sub(c, s, src_ap, src_bufs, nchunks, dst, dst_bufs, evac_dve=False, defer_evac=False):
        bk = take_banks(1)[0]
        pv = banks[bk][:].bitcast(BF16)

        def fn(e):
            ins = None
            for ch in range(nchunks):
                ins = e.transpose(out=pv[:, ch * 128:ch * 128 + c.R], in_=src_ap[:, ch * 128:(ch + 1) * 128], identity=identb[0:c.R, 0:c.R])
            return ins
        PE(fn, reads=list(src_bufs) + [ident_b], writes=[banks_b[bk]])
        pv3 = pv[:, 0:nchunks * 128].rearrange("p (k t) -> p k t", t=128)
        def evac():
            if evac_dve:
                DVE(lambda e: e.tensor_copy(out=dst[:, 0:nchunks, s * 128:s * 128 + c.R], in_=pv3[:, :, 0:c.R]), reads=[banks_b[bk]], writes=list(dst_bufs))
            else:
                ACT(lambda e: e.copy(out=dst[:, 0:nchunks, s * 128:s * 128 + c.R], in_=pv3[:, :, 0:c.R]), reads=[banks_b[bk]], writes=list(dst_bufs))
        if defer_evac:
            return evac
        evac()
        return None

    stg_rr = [0]

    pre_state = {}

    def state_dma(src, name, r0, rn, c0, wg_):
        bi_ = stg_rr[0]; stg_rr[0] = 1 - bi_
        stg = tns[bi_]; stg_b = tns_b[bi_]
        DMA("pool", f"st_in{bi_}", lambda e, r0=r0, rn=rn, c0=c0, wg_=wg_, stg=stg: e.dma_start(out=stg[0:rn, 0:wg_], in_=src[r0:r0 + rn, c0:c0 + wg_]),
            writes=[stg_b])
        return stg, stg_b

    def load_state(src, nrows, width, carr, carr_b, name=None):
        rt = [(0, min(128, nrows))] + ([(128, nrows - 128)] if nrows > 128 else [])
        for c0 in range(0, width, D):
            wg_ = min(D, width - c0)
            for (r0, rn) in rt:
                if (name, r0, c0) in pre_state:
                    stg, stg_b = pre_state.pop((name, r0, c0))
                else:
                    stg, stg_b = state_dma(src, name, r0, rn, c0, wg_)
                for ci in range(wg_ // 128):
                    ch = c0 // 128 + ci
                    bk = take_banks(1)[0]
                    PE(lambda e, ci=ci, bk=bk, rn=rn, stg=stg: e.transpose(out=banks[bk][:, 0:rn], in_=stg[0:rn, ci * 128:(ci + 1) * 128], identity=identf[0:rn, 0:rn]),
                       reads=[stg_b, ident_b], writes=[banks_b[bk]])
                    DVE(lambda e, ch=ch, bk=bk, r0=r0, rn=rn: e.tensor_copy(out=carr[:, ch, r0:r0 + rn], in_=banks[bk][:, 0:rn]),
                        reads=[banks_b[bk]], writes=[carr_b[ch]])

    def s_pre(c, what):
        if c.kind != "sample":
            return
        if what == "1A":
            load_state(sca, SPC * 2, D, cA_s, cA_sb, name="sca")
        elif what == "1P":
            load_state(spl, SPC * 15, D, cP_s, cP_sb, name="spl")
        elif what == "3":
            load_state(sff, SPC * 2, DFF, cF_s, cF_sb, name="sff")

    def s_norm_T_sub(c, s, gidx, defer_evac=False):
        xi = norm_to_bf16(c, s, gidx)
        ev = transposes_sub(c, s, xb[xi][0:c.R, :], [xb_b[xi]], 8, c.actT, c.actT_sub(s), defer_evac=defer_evac)
        return [ev] if ev is not None else []

    def s_load_sub(c, s):
        R = c.R
        DMA(DQ(c), f"x_in_{c.xkey(s)}", lambda e, s=s: e.dma_start(out=c.xres(s), in_=c.x_src[s * R:(s + 1) * R, :]), writes=[c.xres_b[s]])

    def s_1A(c, j, wv, wbuf):
        L, TOK = c.L, c.TOK
        for cc in range(2):
            ch = 2 * j + cc
            bks = take_banks(3)

            def fn(e, cc=cc, bks=bks):
                ins = None
                for s in range(3):
                    for k in range(8):
                        ins = e.matmul(banks[bks[s]][:, 0:TOK], lhsT=wv[:, k, s * 256 + cc * 128:s * 256 + (cc + 1) * 128], rhs=c.actT[:, k, :],
                                       start=(k == 0), stop=(k == 7))
                return ins
            PE(fn, reads=[wbuf] + list(c.actT_b), writes=[banks_b[b] for b in bks])
            bB, bC, bH = bks
            t_c = take_tmp(); t_e = take_tmp(); t_v = take_tmp()
            ACT(lambda e, t_c=t_c, bC=bC: e.copy(out=temps[t_c][:, 0:TOK], in_=banks[bC][:, 0:TOK]), reads=[banks_b[bC]], writes=[temps_b[t_c]])
            ev = ext_view(c, t_e, 2)
            PO(c)(lambda e, ev=ev, ch=ch: e.tensor_copy(out=ev[:, :, 0:2], in_=carry_view(c, c.cA, ch, 2)), reads=[c.cA_b[ch]], writes=[temps_b[t_e]])
            DVE(lambda e, ev=ev, bH=bH, t_c=t_c: e.tensor_tensor(out=ev[:, :, 2:2 + L], in0=tok_view(c, banks[bH][:, 0:TOK]),
                                                                 in1=tok_view(c, temps[t_c][:, 0:TOK]), op=ALU.mult),
                reads=[banks_b[bH], temps_b[t_c]], writes=[temps_b[t_e]])
            PO(c)(lambda e, ev=ev, ch=ch: e.tensor_copy(out=carry_view(c, c.cA, ch, 2), in_=ev[:, :, L:L + 2]), reads=[temps_b[t_e]], writes=[c.cA_b[ch]])
            vv = tok_view(c, temps[t_v][:, 0:TOK])
            ACT(lambda e, ev=ev, vv=vv, ch=ch: e.activation(out=vv, in_=ev[:, :, 0:L], func=AF.Copy, scale=vecT[:, CAW0 + ch:CAW0 + ch + 1]),
                reads=[temps_b[t_e], vecT_b], writes=[temps_b[t_v]])
            for k in (1, 2):
                DVE(lambda e, ev=ev, vv=vv, ch=ch, k=k: e.scalar_tensor_tensor(out=vv, in0=ev[:, :, k:k + L], scalar=vecT[:, CAW0 + k * 8 + ch:CAW0 + k * 8 + ch + 1],
                                                                                 in1=vv, op0=ALU.mult, op1=ALU.add),
                    reads=[temps_b[t_e], temps_b[t_v], vecT_b], writes=[temps_b[t_v]])
            DVE(lambda e, ch=ch, bB=bB, t_v=t_v: e.tensor_tensor(out=c.za[:, ch, :], in0=banks[bB][:, 0:TOK], in1=temps[t_v][:, 0:TOK], op=ALU.mult),
                reads=[banks_b[bB], temps_b[t_v]], writes=c.buf1(ch))

    def s_1P(c, wv, wbuf, chs=range(8)):
        L, TOK = c.L, c.TOK
        for ch in chs:
            g = ch // 2
            w = POOLW[g]
            bk = take_banks(1)[0]

            def fn(e, ch=ch, bk=bk):
                ins = None
                for k in range(8):
                    ins = e.matmul(banks[bk][:, 0:TOK], lhsT=wv[:, k, ch * 128:(ch + 1) * 128], rhs=c.actT[:, k, :], start=(k == 0), stop=(k == 7))
                return ins
            PE(fn, reads=[wbuf] + list(c.actT_b), writes=[banks_b[bk]])
            t_u = take_tmp()
            uv = ext_view(c, t_u, 15)
            PO(c)(lambda e, uv=uv, ch=ch: e.tensor_copy(out=uv[:, :, 0:15], in_=carry_view(c, c.cP, ch, 15)), reads=[c.cP_b[ch]], writes=[temps_b[t_u]])
            ACT(lambda e, uv=uv, bk=bk: e.copy(out=uv[:, :, 15:15 + L], in_=tok_view(c, banks[bk][:, 0:TOK])), reads=[banks_b[bk]], writes=[temps_b[t_u]])
            PO(c)(lambda e, uv=uv, ch=ch: e.tensor_copy(out=carry_view(c, c.cP, ch, 15), in_=uv[:, :, L:L + 15]), reads=[temps_b[t_u]], writes=[c.cP_b[ch]])
            prev = uv; prev_t = t_u
            sh = 1
            while sh < w:
                t_s = take_tmp()
                sv = ext_view(c, t_s, 15)
                lo = 2 * sh - 1
                (PO(c) if sh == 1 else DVE)(lambda e, sv=sv, prev=prev, lo=lo, sh=sh: e.tensor_tensor(out=sv[:, :, lo:15 + L], in0=prev[:, :, lo:15 + L],
                                                                                                       in1=prev[:, :, lo - sh:15 + L - sh], op=ALU.add),
                                           reads=[temps_b[prev_t]], writes=[temps_b[t_s]])
                prev = sv; prev_t = t_s
                sh *= 2
            DVE(lambda e, prev=prev, uv=uv, ch=ch, w=w: e.scalar_tensor_tensor(out=tok_view(c, c.dd[:, ch, :]), in0=prev[:, :, 15:15 + L], scalar=1.0 / w,
                                                                                in1=uv[:, :, 15:15 + L], op0=ALU.mult, op1=ALU.subtract),
                reads=[temps_b[prev_t], temps_b[t_u]], writes=c.buf1(8 + ch))
            if c.first:
                t_f = take_tmp()
                DVE(lambda e, prev=prev, g=g, t_f=t_f: e.tensor_tensor(out=temps[t_f][:, 0:16], in0=prev[:, 0, 15:31], in1=rc16[:, g, :], op=ALU.mult),
                    reads=[temps_b[prev_t], rc16_b], writes=[temps_b[t_f]])
                DVE(lambda e, uv=uv, ch=ch, t_f=t_f: e.tensor_tensor(out=c.dd[:, ch, 0:16], in0=temps[t_f][:, 0:16], in1=uv[:, 0, 15:31], op=ALU.subtract),
                    reads=[temps_b[t_f], temps_b[t_u]], writes=c.buf1(8 + ch))

    def s_1G(c, jg, wg, wa, wp, wbufs):
        TOK = c.TOK
        for cc in range(4):
            ch = 4 * jg + cc
            g = ch // 2
            bks = take_banks(4)

            def fn(e, cc=cc, ch=ch, g=g, bks=bks):
                ins = None
                for s in range(2):
                    for k in range(8):
                        ins = e.matmul(banks[bks[s]][:, 0:TOK], lhsT=wg[:, k, s * 512 + cc * 128:s * 512 + (cc + 1) * 128], rhs=c.actT[:, k, :],
                                       start=(k == 0), stop=(k == 7))
                for k in range(8):
                    ins = e.matmul(banks[bks[2]][:, 0:TOK], lhsT=wa[:, k, ch * 128:(ch + 1) * 128], rhs=c.za[:, k, :], start=(k == 0), stop=(k == 7))
                for k in range(2):
                    ins = e.matmul(banks[bks[3]][:, 0:TOK], lhsT=wp[:, g * 2 + k, (ch % 2) * 128:(ch % 2 + 1) * 128], rhs=c.dd[:, g * 2 + k, :],
                                   start=(k == 0), stop=(k == 1))
                return ins
            PE(fn, reads=list(wbufs) + list(c.actT_b) + c.bufs(0, 16), writes=[banks_b[b] for b in bks])
            t_a = take_tmp(); t_p = take_tmp()
            ACT(lambda e, t_a=t_a, b=bks[0]: e.activation(out=temps[t_a][:, 0:TOK], in_=banks[b][:, 0:TOK], func=AF.Sigmoid), reads=[banks_b[bks[0]]], writes=[temps_b[t_a]])
            ACT(lambda e, t_p=t_p, b=bks[1]: e.activation(out=temps[t_p][:, 0:TOK], in_=banks[b][:, 0:TOK], func=AF.Sigmoid), reads=[banks_b[bks[1]]], writes=[temps_b[t_p]])
            DVE(lambda e, t_a=t_a, b=bks[2]: e.tensor_tensor(out=temps[t_a][:, 0:TOK], in0=banks[b][:, 0:TOK], in1=temps[t_a][:, 0:TOK], op=ALU.mult),
                reads=[banks_b[bks[2]], temps_b[t_a]], writes=[temps_b[t_a]])
            DVE(lambda e, t_p=t_p, b=bks[3], ch=ch: e.scalar_tensor_tensor(out=temps[t_p][:, 0:TOK], in0=banks[b][:, 0:TOK], scalar=vecT[:, PSC0 + ch:PSC0 + ch + 1],
                                                                             in1=temps[t_p][:, 0:TOK], op0=ALU.mult, op1=ALU.mult),
                reads=[banks_b[bks[3]], temps_b[t_p], vecT_b], writes=[temps_b[t_p]])
            DVE(lambda e, t_a=t_a, t_p=t_p, ch=ch: e.tensor_tensor(out=c.mg[:, ch, :], in0=temps[t_a][:, 0:TOK], in1=temps[t_p][:, 0:TOK], op=ALU.add),
                reads=[temps_b[t_a], temps_b[t_p]], writes=c.buf1(16 + ch))

    def s_tm(c, lhs, lhs_b, nk, wv_fn, w_bufs, gidx, post=None, lag=1):
        R = c.R
        for s in range(c.NSUB + lag):
            if s >= c.NSUB:
                if post is not None and s - lag >= 0:
                    for ev in (post(s - lag) or []):
                        ev()
                continue
            bks = take_banks(2)

            def fn(e, s=s, bks=bks):
                ins = None
                for hf in range(2):
                    for k in range(nk):
                        ins = e.matmul(banks[bks[hf]][0:R, :], lhsT=lhs[:, k, s * 128:s * 128 + R], rhs=wv_fn(k, hf), start=(k == 0), stop=(k == nk - 1))
                return ins
            PE(fn, reads=list(w_bufs) + list(lhs_b), writes=[banks_b[b] for b in bks])
            late = []
            if s >= lag and post is not None:
                late = post(s - lag) or []
            sa = take_stat(); sbb = take_stat()
            ti_ = tn_rr[0]; tn_rr[0] = 1 - ti_
            tn = tns[ti_]; tn_b = tns_b[ti_]
            ACT(lambda e, sa=sa: e.memzero(stats[:, sa:sa + 1]), writes=[stats_b[sa]])
            ACT(lambda e, sbb=sbb: e.memzero(stats[:, sbb:sbb + 1]), writes=[stats_b[sbb]])
            ACT(lambda e, sa=sa, b=bks[0]: e.activation(out=junk[0:R, 0:512], in_=banks[b][0:R, :], func=AF.Square, scale=1.0 / 32.0, accum_out=stats[0:R, sa:sa + 1]),
                reads=[banks_b[bks[0]]], writes=[junk_b, stats_b[sa]])
            DVE(lambda e, b=bks[1], tn=tn: e.tensor_tensor(out=tn[0:R, 512:1024], in0=banks[b][0:R, :], in1=gbc[0:R, gidx, 512:1024], op=ALU.mult),
                reads=[banks_b[bks[1]], gbc_b[gidx]], writes=[tn_b])
            ACT(lambda e, sbb=sbb, b=bks[1]: e.activation(out=junk[0:R, 512:1024], in_=banks[b][0:R, :], func=AF.Square, scale=1.0 / 32.0, accum_out=stats[0:R, sbb:sbb + 1]),
                reads=[banks_b[bks[1]]], writes=[junk_b, stats_b[sbb]])
            DVE(lambda e, b=bks[0], tn=tn: e.tensor_tensor(out=tn[0:R, 0:512], in0=banks[b][0:R, :], in1=gbc[0:R, gidx, 0:512], op=ALU.mult),
                reads=[banks_b[bks[0]], gbc_b[gidx]], writes=[tn_b])
            DVE(lambda e, sa=sa, sbb=sbb: e.tensor_tensor(out=stats[0:R, sa:sa + 1], in0=stats[0:R, sa:sa + 1], in1=stats[0:R, sbb:sbb + 1], op=ALU.add),
                reads=[stats_b[sa], stats_b[sbb]], writes=[stats_b[sa]])
            ACT(lambda e, sa=sa: e.activation(out=stats[0:R, sa:sa + 1], in_=stats[0:R, sa:sa + 1], func=AF.Sqrt, bias=epsT[0:R, 0:1], scale=1.0),
                reads=[stats_b[sa], epsT_b], writes=[stats_b[sa]])
            DVE(lambda e, sa=sa: e.reciprocal(out=stats[0:R, sa:sa + 1], in_=stats[0:R, sa:sa + 1]), reads=[stats_b[sa]], writes=[stats_b[sa]])
            DVE(lambda e, s=s, sa=sa, tn=tn: e.scalar_tensor_tensor(out=c.xres(s), in0=tn[0:R, :], scalar=stats[0:R, sa:sa + 1], in1=c.xres(s),
                                                                     op0=ALU.mult, op1=ALU.add),
                reads=[c.xres_b[s], tn_b, stats_b[sa]], writes=[c.xres_b[s]])
            for ev in late:
                ev()

    def s_3(c, j, nch, wu, wbuf):
        L, TOK = c.L, c.TOK
        pend = c.__dict__.setdefault("s3_pend", [])
        for cc in range(nch):
            ch = 4 * j + cc
            bks = take_banks(2)

            def fn(e, cc=cc, bks=bks):
                ins = None
                for s in range(2):
                    for k in range(8):
                        ins = e.matmul(banks[bks[s]][:, 0:TOK], lhsT=wu[:, k, s * 512 + cc * 128:s * 512 + (cc + 1) * 128], rhs=c.actT[:, k, :],
                                       start=(k == 0), stop=(k == 7))
                return ins
            PE(fn, reads=[wbuf] + list(c.actT_b), writes=[banks_b[b] for b in bks])
            bA, bG = bks
            t_e = take_tmp(); t_v = take_tmp()
            ev = ext_view(c, t_e, 2)
            PO(c)(lambda e, ev=ev, ch=ch: e.tensor_copy(out=ev[:, :, 0:2], in_=carry_view(c, c.cF, ch, 2)), reads=[c.cF_b[ch]], writes=[temps_b[t_e]])
            ACT(lambda e, ev=ev, bA=bA: e.copy(out=ev[:, :, 2:2 + L], in_=tok_view(c, banks[bA][:, 0:TOK])), reads=[banks_b[bA]], writes=[temps_b[t_e]])
            PO(c)(lambda e, ev=ev, ch=ch: e.tensor_copy(out=carry_view(c, c.cF, ch, 2), in_=ev[:, :, L:L + 2]), reads=[temps_b[t_e]], writes=[c.cF_b[ch]])
            vv = tok_view(c, temps[t_v][:, 0:TOK])
            ACT(lambda e, ev=ev, vv=vv, ch=ch: e.activation(out=vv, in_=ev[:, :, 0:L], func=AF.Copy, scale=vecT[:, FCW0 + ch:FCW0 + ch + 1]),
                reads=[temps_b[t_e], vecT_b], writes=[temps_b[t_v]])
            for k in (1, 2):
                DVE(lambda e, ev=ev, vv=vv, ch=ch, k=k: e.scalar_tensor_tensor(out=vv, in0=ev[:, :, k:k + L], scalar=vecT[:, FCW0 + k * NFC + ch:FCW0 + k * NFC + ch + 1],
                                                                                 in1=vv, op0=ALU.mult, op1=ALU.add),
                    reads=[temps_b[t_e], temps_b[t_v], vecT_b], writes=[temps_b[t_v]])
            s_3_flush(c)
            pend.append((ch, bG, t_v))
        s_3_flush(c)

    def s_3_flush(c):
        TOK = c.TOK
        pend = c.__dict__.setdefault("s3_pend", [])
        while pend:
            ch, bG, t_v = pend.pop(0)
            ACT(lambda e, t_v=t_v: e.activation(out=temps[t_v][:, 0:TOK], in_=temps[t_v][:, 0:TOK], func=AF.Gelu_apprx_tanh), reads=[temps_b[t_v]], writes=[temps_b[t_v]])
            DVE(lambda e, ch=ch, bG=bG, t_v=t_v: e.tensor_tensor(out=c.ff[:, ch, :], in0=banks[bG][:, 0:TOK], in1=temps[t_v][:, 0:TOK], op=ALU.mult),
                reads=[banks_b[bG], temps_b[t_v]], writes=c.buf1(ch))

    def s_5pre_sub(c, s):
        R = c.R
        if True:
            xi = xb_rr[0]
            xb_rr[0] = (xi + 1) % 2
            ACT(lambda e, s=s, xi=xi: e.copy(out=xb[xi][0:R, :], in_=c.xres(s)), reads=[c.xres_b[s]], writes=[xb_b[xi]])
            ev1 = transposes_sub(c, s, xb[xi][0:R, :], [xb_b[xi]], 8, c.actT, c.actT_sub(s), defer_evac=True)
            DMA(DQ(c), "p_in", lambda e, s=s: e.dma_start(out=pf[0:R, :], in_=c.p_src[s * R:(s + 1) * R, :]), writes=[pf_b])
            DVE(lambda e: e.tensor_copy(out=pb[0:R, :], in_=pf[0:R, :]), reads=[pf_b], writes=[pb_b])
            ev2 = transposes_sub(c, s, pb[0:R, :], [pb_b], 2, c.pT, c.pT_b, defer_evac=True)
            return [ev1, ev2]

    def s_5(c, wgt, wpr, wbuf, post_a=None, post_b=None):
        R = c.R
        for s in range(c.NSUB):
            grp = []
            for hf in range(2):
                bks = take_banks(2)

                def fn(e, s=s, hf=hf, bks=bks):
                    ins = None
                    for k in range(8):
                        ins = e.matmul(banks[bks[0]][0:R, :], lhsT=c.actT[:, k, s * 128:s * 128 + R], rhs=wgt[:, k, hf * 512:(hf + 1) * 512], start=(k == 0), stop=(k == 7))
                    for k in range(2):
                        ins = e.matmul(banks[bks[1]][0:R, :], lhsT=c.pT[:, k, s * 128:s * 128 + R], rhs=wpr[:, k, hf * 512:(hf + 1) * 512], start=(k == 0), stop=(k == 1))
                    return ins
                PE(fn, reads=list(wbuf) + list(c.pT_b) + c.actT_sub(s), writes=[banks_b[b] for b in bks])
                grp.append(bks)
            if post_a is not None:
                if s >= 1:
                    post_a(s - 1)
                if s == c.NSUB - 1:
                    post_a(s)
            yo = tn_rr[0]; tn_rr[0] = 1 - yo
            for hf in range(2):
                bks = grp[hf]
                t_g = sg_rr[0]; sg_rr[0] = 1 - t_g
                ACT(lambda e, t_g=t_g, b=bks[0]: e.activation(out=sg5[t_g][0:R, :], in_=banks[b][0:R, :], func=AF.Sigmoid), reads=[banks_b[bks[0]]], writes=[sg5_b[t_g]])
                DVE(lambda e, t_g=t_g, b=bks[1]: e.tensor_tensor(out=sg5[t_g][0:R, :], in0=banks[b][0:R, :], in1=sg5[t_g][0:R, :], op=ALU.mult),
                    reads=[banks_b[bks[1]], sg5_b[t_g]], writes=[sg5_b[t_g]])
                PO(c)(lambda e, s=s, hf=hf, t_g=t_g, yo=yo: e.tensor_tensor(out=tns[yo][0:R, hf * 512:(hf + 1) * 512], in0=c.xres(s)[:, hf * 512:(hf + 1) * 512],
                                                                           in1=sg5[t_g][0:R, :], op=ALU.add),
                      reads=[c.xres_b[s], sg5_b[t_g]], writes=[tns_b[yo]])
            DMA(DQ(c), f"y_out_{yo}", lambda e, s=s, yo=yo: e.dma_start(out=c.y_dst[s * R:(s + 1) * R, :], in_=tns[yo][0:R, :]), reads=[tns_b[yo]], is_output=True)
            if post_b is not None:
                post_b(s)

    preloaded = set()

    def prologue_sub(c, s):
        if (c.key, s) not in preloaded:
            s_load_sub(c, s)
        s_norm_T_sub(c, s, 0)

    def stage_loads(n):
        for s in range(n.NSUB):
            DMA(DQ(n), f"x_in_{n.xkey(s)}", lambda e, s=s: e.dma_start(out=xst(s), in_=n.x_src[s * 128:(s + 1) * 128, :]), writes=xst_b(s))

    def prologue_staged_a(n, s):
        xi = norm_to_bf16(n, s, 0, src=xst(s), src_b=xst_b(s))
        transposes_sub(n, s, xb[xi][0:n.R, :], [xb_b[xi]], 8, n.actT, n.actT_sub(s), evac_dve=True)

    def prologue_staged_b(n, s):
        DVE(lambda e, s=s: e.tensor_copy(out=n.xres(s), in_=xst(s)), reads=xst_b(s), writes=[n.xres_b[s]])

    def emit_group(ctxs, nxt):
        stage[0] = "0"
        for c in ctxs:
            if c.kind == "sample" or c.first:
                pend_ev = []
                for s in range(c.NSUB):
                    if (c.key, s) not in preloaded:
                        s_load_sub(c, s)
                    evs = s_norm_T_sub(c, s, 0, defer_evac=True)
                    for ev in pend_ev:
                        ev()
                    pend_ev = evs
                for ev in pend_ev:
                    ev()
                if c.first:
                    emit_vecT()
        for j in range(4):
            stage[0] = f"1A{j}"
            gj, si = acquire(f"A{j}")
            if j == 0:
                gj_u, s_uu = acquire("U")
                wvu = sview(slots[s_uu], 8, D)
            wv = sview(slots[si], 8, 768)
            for c in ctxs:
                if j == 0:
                    s_pre(c, "1A")
                s_1A(c, j, wv, slots_b[si])
            release(gj)
            for c in ctxs:
                if j == 0:
                    s_pre(c, "1P")
                s_1P(c, wvu, slots_b[s_uu], chs=range(2 * j, 2 * j + 2))
        release(gj_u)
        stage[0] = "1G"
        gj_g, s_g = acquire("G0")
        gj_ap, s_ap = acquire("AO")
        gj_pw, s_pw = acquire("PW")
        wa = sview(slots[s_ap], 8, D)
        wp = sview(slots[s_pw], 8, 256)
        for jg in range(2):
            if jg == 1:
                release(gj_g)
                gj_g, s_g = acquire("G1")
            wg = sview(slots[s_g], 8, 1024)
            for c in ctxs:
                s_1G(c, jg, wg, wa, wp, [slots_b[s_g], slots_b[s_ap], slots_b[s_pw]])
        release(gj_g)
        release(gj_ap)
        release(gj_pw)
        stage[0] = "2"
        gj, s_o = acquire("O")
        wo = sview(slots[s_o], 8, D)
        for c in ctxs:
            s_tm(c, c.mg, c.bufs(16, 24), 8, lambda k, hf: wo[:, k, hf * 512:(hf + 1) * 512], [slots_b[s_o]], 1,
                 post=lambda s, c=c: s_norm_T_sub(c, s, 2, defer_evac=True), lag=2)
        release(gj)
        for c in ctxs:
            s_pre(c, "3")
        for j in range(6):
            nch = 4 if j < 5 else 2
            stage[0] = f"3_{j}"
            gj, s_u = acquire(f"UP{j}")
            wu = sview(slots[s_u], 8, 1024)
            for c in ctxs:
                s_3(c, j, nch, wu, slots_b[s_u])
            release(gj)
        for c in ctxs:
            s_3_flush(c)
        stage[0] = "4"
        gd = [acquire(f"D{kb}") for kb in range(3)]
        wd = [sview(slots[gd[kb][1]], DKB[kb][1], D) for kb in range(3)]
        for c in ctxs:
            s_tm(c, c.ff, c.bufs(0, NFC), NFC, lambda k, hf: wd[k // 8][:, k % 8, hf * 512:(hf + 1) * 512], [slots_b[g_[1]] for g_ in gd], 3,
                 post=lambda s, c=c: s_5pre_sub(c, s), lag=1)
        for g_ in gd:
            release(g_[0])
        stage[0] = "5"
        gj_e, s_e = acquire("E1")
        gj_e2, s_e2 = acquire("E2")
        wgt = sview(slots[s_e], 8, D)
        wpr = sview(slots[s_e2], 2, D)
        for c in ctxs:
            if c.kind == "prompt" and nxt is not None:
                stage_loads(nxt)
                s_5(c, wgt, wpr, [slots_b[s_e], slots_b[s_e2]], post_a=lambda s: prologue_staged_a(nxt, s), post_b=lambda s: prologue_staged_b(nxt, s))
            else:
                s_5(c, wgt, wpr, [slots_b[s_e], slots_b[s_e2]])
        release(gj_e)
        release(gj_e2)

    def emit_state_out(carr, carr_b, nch, nrows, dst):
        rts = [(0, min(128, nrows))] + ([(128, nrows - 128)] if nrows > 128 else [])
        for (r0, rn) in rts:
            for g0 in range(0, nch, 2):
                gn = min(2, nch - g0)
                bk = take_banks(1)[0]

                def fn(e, g0=g0, gn=gn, bk=bk, r0=r0, rn=rn):
                    ins = None
                    for i in range(gn):
                        ins = e.transpose(out=banks[bk][0:rn, i * 128:(i + 1) * 128], in_=carr[:, g0 + i, r0:r0 + rn], identity=identf[:])
                    return ins
                PE(fn, reads=[carr_b[g0 + i] for i in range(gn)] + [ident_b], writes=[banks_b[bk]])
                ti = take_tmp()
                DVE(lambda e, bk=bk, ti=ti, gn=gn, rn=rn: e.tensor_copy(out=temps[ti][0:rn, 0:gn * 128], in_=banks[bk][0:rn, 0:gn * 128]), reads=[banks_b[bk]], writes=[temps_b[ti]])
                DMA("pool", "st_out", lambda e, ti=ti, g0=g0, gn=gn, r0=r0, rn=rn: e.dma_start(out=dst[r0:r0 + rn, g0 * 128:(g0 + gn) * 128], in_=temps[ti][0:rn, 0:gn * 128]),
                    reads=[temps_b[ti]], is_output=True)

    pctx = [make_prompt_ctx(t) for t in range(nt_prompt)]
    sctx = make_sample_ctx() if with_sample else None
    for c0 in [pctx[0]] + ([sctx] if with_sample else []):
        for s in range(c0.NSUB):
            s_load_sub(c0, s)
            preloaded.add((c0.key, s))
    if with_sample:
        pre_state[("sca", 0, 0)] = state_dma(sca, "sca", 0, SPC * 2, 0, D)
        pre_state[("spl", 0, 0)] = state_dma(spl, "spl", 0, 128, 0, D)
    pump_loads()
    for t in range(nt_prompt):
        ctxs = [pctx[t]]
        if t == 0 and with_sample:
            ctxs.append(sctx)
        emit_group(ctxs, pctx[t + 1] if t + 1 < nt_prompt else None)
        if t == 0 and with_sample:
            emit_state_out(cA_s, cA_sb, 8, SPC * 2, ncs)
            emit_state_out(cP_s, cP_sb, 8, SPC * 15, nps)
            emit_state_out(cF_s, cF_sb, NFC, SPC * 2, nfs)
    emit_state_out(cA_p, cA_pb, 8, 2, ncp)
    emit_state_out(cP_p, cP_pb, 8, 15, npp)
    emit_state_out(cF_p, cF_pb, NFC, 2, nfp)

    eng_names = {"pe": "tensor", "act": "scalar", "dve": "vector", "pool": "gpsimd", "sp": "sync"}
    sems = {}
    for key in list(T.ENG) + list(T.dma_count.keys()):
        sems[key] = es.enter_context(nc.semaphore(f"s_{key}"))
    final = {}
    for tok in T.out_dma:
        final[tok[0]] = max(final.get(tok[0], 0), tok[1])

    block = es.enter_context(nc.Block())

    class _Cnt:
        def __init__(self, e):
            self._e = e; self.n = 0

        def matmul(self, *a, **k):
            self.n += 1
            return self._e.matmul(*a, **k)

        def transpose(self, *a, **k):
            self.n += 1
            return self._e.transpose(*a, **k)

    pe_counts = []

    def make_stream(ename):
        def body(e):
            for waits, fn, inc in T.streams[ename]:
                for key, val in waits:
                    e.wait_ge(sems[key], val)
                if ename == "pe":
                    ce = _Cnt(e)
                    ins = fn(ce)
                    pe_counts.append(ce.n)
                else:
                    ins = fn(e)
                ins.then_inc(sems[inc[0]], inc[1])
            if ename == "sp":
                for key, val in final.items():
                    e.wait_ge(sems[key], val)
        return body

    for ename in T.ENG:
        getattr(block, eng_names[ename])(make_stream(ename))
    es.close()
    nc._pe_info = list(zip(pe_labels, pe_counts))
    return nc


_CACHE = {}


def _get_nc(nt_prompt, with_sample):
    key = (nt_prompt, with_sample)
    if key not in _CACHE:
        _CACHE[key] = build_program(nt_prompt, with_sample)
    return _CACHE[key]


def make_in_maps(inputs, nt_prompt=4, ncores=NCORES):
    f = lambda a: np.ascontiguousarray(np.asarray(a, dtype=np.float32))
    ntok = nt_prompt * 512
    shared = {
        "g_pre_mix": f(inputs["g_pre_mix"][0]), "w_in": f(inputs["w_in"][0]), "conv_a_w": f(inputs["conv_a_w"][0]),
        "w_a_out": f(inputs["w_a_out"][0]), "pool_w": f(inputs["pool_w"][0]), "pool_scale": f(inputs["pool_scale"][0]),
        "w_o": f(inputs["w_o"][0]), "g_post_mix": f(inputs["g_post_mix"][0]), "g_pre_ffn": f(inputs["g_pre_ffn"][0]),
        "w_up": f(inputs["w_up"][0]), "ffn_conv_w": f(inputs["ffn_conv_w"][0]), "w_down": f(inputs["w_down"][0]),
        "g_post_ffn": f(inputs["g_post_ffn"][0]), "w_ple_proj": f(inputs["w_ple_proj"][0]), "w_ple_gate": f(inputs["w_ple_gate"][0]),
    }
    maps = []
    for b in range(ncores):
        m = dict(shared)
        m["xp"] = f(inputs["x_prompt"][b, :ntok])
        m["pp"] = f(inputs["p_prompt"][0, b, :ntok])
        sl = slice(b * SPC, (b + 1) * SPC)
        m["xs"] = f(inputs["x_sample"][sl]).reshape(SPC * DEC_S, D)
        m["ps"] = f(inputs["p_sample"][0, sl]).reshape(SPC * DEC_S, PLE)
        m["sca"] = f(inputs["state_conv_a"][0, sl]).reshape(SPC * 2, D)
        m["spl"] = f(inputs["state_pool"][0, sl]).reshape(SPC * 15, D)
        m["sff"] = f(inputs["state_ffn_conv"][0, sl]).reshape(SPC * 2, DFF)
        maps.append(m)
    return maps


def kernel(**inputs):
    nc = _get_nc(4, True)
    maps = make_in_maps(inputs)
    res = run_bass_kernel_spmd(nc, maps, core_ids=list(range(NCORES)))
    r = res.results
    y_prompt = np.stack([r[b]["yp"] for b in range(NCORES)], axis=0).astype(np.float32)
    y_sample = np.concatenate([r[b]["ys"].reshape(SPC, DEC_S, D) for b in range(NCORES)], axis=0).astype(np.float32)
    ncp = np.stack([r[b]["ncp"] for b in range(NCORES)], axis=0)[None].astype(np.float32)
    npp = np.stack([r[b]["npp"] for b in range(NCORES)], axis=0)[None].astype(np.float32)
    nfp = np.stack([r[b]["nfp"] for b in range(NCORES)], axis=0)[None].astype(np.float32)
    ncs = np.concatenate([r[b]["ncs"].reshape(SPC, 2, D) for b in range(NCORES)], axis=0)[None].astype(np.float32)
    nps = np.concatenate([r[b]["nps"].reshape(SPC, 15, D) for b in range(NCORES)], axis=0)[None].astype(np.float32)
    nfs = np.concatenate([r[b]["nfs"].reshape(SPC, 2, DFF) for b in range(NCORES)], axis=0)[None].astype(np.float32)
    return (y_prompt, y_sample, ncp, npp, nfp, ncs, nps, nfs)
```

```python
import numpy as np
import concourse.bass as bass
import concourse.mybir as mybir
from concourse.bass_utils import run_bass_kernel_spmd

F32 = mybir.dt.float32
BF16 = mybir.dt.bfloat16
I32 = mybir.dt.int32
AF = mybir.ActivationFunctionType
ALU = mybir.AluOpType

D = 1024
DFF = 2816
NFC = DFF // 128
PLE = 256
EPS = 1e-6
NCORES = 8
SEQ = 2048
DEC_B = 128
DEC_S = 4
SPC = DEC_B // NCORES
POOLW = (2, 4, 8, 16)

SLOT_ELEMS = 8192
NSLOTS = 4
NTEMPS = 12
TEMPW = 528


class Buf:
    __slots__ = ("name", "w", "r", "excl")

    def __init__(self, name, excl=False):
        self.name = name
        self.w = None
        self.r = {}
        self.excl = excl


class Tracker:
    ENG = ("pe", "act", "dve", "pool", "sp")

    def __init__(self):
        self.streams = {e: [] for e in self.ENG}
        self.count = {e: 0 for e in self.ENG}
        self.waited = {e: {} for e in self.ENG}
        self.dma_count = {}
        self.out_dma = []

    def dma_sem(self, name):
        if name not in self.dma_count:
            self.dma_count[name] = 0
        return name

    def emit(self, eng, fn, reads=(), writes=(), dma=None, is_output=False):
        need = {}

        def add(tok, kind):
            if tok is None:
                return
            key, val, teng, isdma = tok
            if not isdma and teng == eng:
                if eng == "pe":
                    return
                if kind in ("WAR", "RAR", "WAW"):
                    return
            if need.get(key, 0) < val:
                need[key] = val

        for b in reads:
            add(b.w, "RAW")
            if b.excl:
                for t in b.r.values():
                    add(t, "RAR")
        for b in writes:
            add(b.w, "WAW")
            for t in b.r.values():
                add(t, "WAR")
        waits = []
        wd = self.waited[eng]
        for key, val in need.items():
            if wd.get(key, 0) >= val:
                continue
            wd[key] = val
            waits.append((key, val))
        if dma is not None:
            self.dma_count[dma] += 16
            tok = (dma, self.dma_count[dma], eng, True)
            inc = (dma, 16)
            if is_output:
                self.out_dma.append(tok)
        else:
            self.count[eng] += 1
            tok = (eng, self.count[eng], eng, False)
            inc = (eng, 1)
        self.streams[eng].append((waits, fn, inc))
        for b in reads:
            old = b.r.get(tok[0])
            if old is None or old[1] < tok[1]:
                b.r[tok[0]] = tok
        for b in writes:
            b.w = tok
            b.r = {}
        return tok


def build_program(nt_prompt=4, with_sample=True, debug=False):
    nc = bass.Bass("TRN2", target_bir_lowering=False)
    NTOKP = nt_prompt * 512
    NS = SPC * DEC_S

    def din(name, shape):
        return nc.dram_tensor(name, list(shape), F32, kind="ExternalInput").ap()

    def dout(name, shape):
        return nc.dram_tensor(name, list(shape), F32, kind="ExternalOutput").ap()

    xp = din("xp", [NTOKP, D]); xs = din("xs", [NS, D])
    pp = din("pp", [NTOKP, PLE]); psm = din("ps", [NS, PLE])
    sca = din("sca", [SPC * 2, D]); spl = din("spl", [SPC * 15, D]); sff = din("sff", [SPC * 2, DFF])
    g_pre_mix = din("g_pre_mix", [D]); w_in = din("w_in", [D, 6 * D]); conv_a_w = din("conv_a_w", [3, D])
    w_a_out = din("w_a_out", [D, D]); pool_w = din("pool_w", [4, 256, 256]); pool_scale = din("pool_scale", [D])
    w_o = din("w_o", [D, D]); g_post_mix = din("g_post_mix", [D]); g_pre_ffn = din("g_pre_ffn", [D])
    w_up = din("w_up", [D, 2 * DFF]); ffn_conv_w = din("ffn_conv_w", [3, DFF]); w_down = din("w_down", [DFF, D])
    g_post_ffn = din("g_post_ffn", [D]); w_ple_proj = din("w_ple_proj", [PLE, D]); w_ple_gate = din("w_ple_gate", [D, D])

    yp = dout("yp", [NTOKP, D]); ys = dout("ys", [NS, D])
    ncp = dout("ncp", [2, D]); npp = dout("npp", [15, D]); nfp = dout("nfp", [2, DFF])
    ncs = dout("ncs", [SPC * 2, D]); nps = dout("nps", [SPC * 15, D]); nfs = dout("nfs", [SPC * 2, DFF])

    T = Tracker()
    from contextlib import ExitStack
    es = ExitStack()

    def sb(name, shape, dt=F32):
        return es.enter_context(nc.sbuf_tensor(name, list(shape), dt))

    xres = sb("xres", [128, 4, D]); xres_b = [Buf(f"xres{s}") for s in range(4)]
    xres_s = sb("xres_s", [128, 1, D]); xres_sb = [Buf("xres_s")]
    xb = [sb(f"xb{i}", [128, D], BF16) for i in range(2)]; xb_b = [Buf(f"xb{i}") for i in range(2)]
    actT = sb("actT", [128, 8, 512], BF16); actT_b = [Buf(f"actT_s{s}") for s in range(4)]
    big2 = sb("big", [128, 24 * 512], BF16); big_b = [[Buf(f"big{h}_{c}") for c in range(24)] for h in range(2)]
    big = big2[:, :].rearrange("p (c t) -> p c t", t=512)
    xstage = big2[:, :].bitcast(F32)

    def xst(s):
        return xstage[:, s * 1024:(s + 1) * 1024]

    def xst_b(s):
        return [big_b[h][i] for h in range(2) for i in range(4 * s, 4 * s + 4)]
    pT = sb("pT", [128, 2, 512], BF16); pT_b = [Buf(f"pT{h}") for h in range(2)]
    actT_s = sb("actT_s", [128, 8, 64], BF16); actT_sb = [Buf("actTs")]
    big_s = sb("big_s", [128, 24, 64], BF16); big_sb = [Buf(f"bigs_{c}") for c in range(24)]
    pT_s = sb("pT_s", [128, 2, 64], BF16); pT_sb = Buf("pTs")
    pf = sb("pf", [128, PLE]); pf_b = Buf("pf")
    pb = sb("pb", [128, PLE], BF16); pb_b = Buf("pb")
    temps = [sb(f"tmp{i}", [128, TEMPW]) for i in range(NTEMPS)]; temps_b = [Buf(f"tmp{i}") for i in range(NTEMPS)]
    junk = sb("junk", [128, D], BF16); junk_b = Buf("junk")
    tns = [sb(f"tn{i}", [128, D]) for i in range(2)]; tns_b = [Buf(f"tn{i}") for i in range(2)]
    tn_rr = [0]
    sg5 = [sb(f"sg5_{i}", [128, 512]) for i in range(2)]; sg5_b = [Buf(f"sg5_{i}") for i in range(2)]
    sg_rr = [0]
    stats = sb("stats", [128, 16]); stats_b = [Buf(f"st{i}") for i in range(16)]
    gbc = sb("gbc", [128, 4, D]); gbc_b = [Buf(f"gbc{i}") for i in range(4)]
    vrow = sb("vrow", [128, 128]); vrow_b = Buf("vrow")
    vecT = sb("vecT", [128, 128]); vecT_b = Buf("vecT")
    identf = sb("identf", [128, 128]); identb = sb("identb", [128, 128], BF16); ident_b = Buf("ident")
    iot = sb("iot", [128, 128], I32); iot_b = Buf("iot")
    rc16 = sb("rc16", [128, 4, 16]); rc16_b = Buf("rc16")
    io16 = sb("io16", [128, 16]); io16_b = Buf("io16")
    epsT = sb("epsT", [128, 1]); epsT_b = Buf("epsT")
    slots = [sb(f"slot{i}", [128, SLOT_ELEMS], BF16) for i in range(NSLOTS)]; slots_b = [Buf(f"slot{i}") for i in range(NSLOTS)]
    cA_p = sb("cA_p", [128, 8, 2]); cP_p = sb("cP_p", [128, 8, 15]); cF_p = sb("cF_p", [128, NFC, 2])
    cA_s = sb("cA_s", [128, 8, SPC * 2]); cP_s = sb("cP_s", [128, 8, SPC * 15]); cF_s = sb("cF_s", [128, NFC, SPC * 2])
    cA_pb = [Buf(f"cAp{c}") for c in range(8)]; cP_pb = [Buf(f"cPp{c}") for c in range(8)]; cF_pb = [Buf(f"cFp{c}") for c in range(NFC)]
    cA_sb = [Buf(f"cAs{c}") for c in range(8)]; cP_sb = [Buf(f"cPs{c}") for c in range(8)]; cF_sb = [Buf(f"cFs{c}") for c in range(NFC)]
    strow = tns[1]; strow_b = tns_b[1]

    banks = [es.enter_context(nc.psum_tensor(f"bank{i}", [128, 512], F32)) for i in range(8)]
    banks_b = [Buf(f"bank{i}", excl=True) for i in range(8)]
    bank_rr = [0]

    def take_banks(n):
        i = bank_rr[0]
        if i + n > 8:
            i = 0
        bank_rr[0] = (i + n) % 8
        return list(range(i, i + n))

    tmp_rr = [0]

    def take_tmp():
        i = tmp_rr[0]
        tmp_rr[0] = (i + 1) % NTEMPS
        return i

    st_rr = [0]

    def take_stat():
        i = st_rr[0]
        st_rr[0] = (i + 1) % 16
        return i

    xb_rr = [0]

    def ACT(fn, reads=(), writes=()):
        return T.emit("act", fn, reads, writes)

    def DVE(fn, reads=(), writes=()):
        return T.emit("dve", fn, reads, writes)

    def POOL(fn, reads=(), writes=()):
        return T.emit("pool", fn, reads, writes)

    stage = ["init"]
    pe_labels = []

    def PE(fn, reads=(), writes=()):
        pe_labels.append(stage[0])
        return T.emit("pe", fn, reads, writes)

    def DMA(eng, semname, fn, reads=(), writes=(), is_output=False):
        T.dma_sem(semname)
        return T.emit(eng, fn, reads, writes, dma=semname, is_output=is_output)

    CAW0 = 0
    PSC0 = 24
    FCW0 = 32

    DMA("sp", "c_gbc0", lambda e: e.dma_start(out=gbc[:, 0, :], in_=g_pre_mix.partition_broadcast(128)), writes=[gbc_b[0]])
    DMA("sp", "c_gbc1", lambda e: e.dma_start(out=gbc[:, 1, :], in_=g_post_mix.partition_broadcast(128)), writes=[gbc_b[1]])
    DMA("sp", "c_gbc2", lambda e: e.dma_start(out=gbc[:, 2, :], in_=g_pre_ffn.partition_broadcast(128)), writes=[gbc_b[2]])
    DMA("sp", "c_gbc3", lambda e: e.dma_start(out=gbc[:, 3, :], in_=g_post_ffn.partition_broadcast(128)), writes=[gbc_b[3]])
    POOL(lambda e: e.memset(vrow[:], 0.0), writes=[vrow_b])
    DMA("sp", "c_vrow", lambda e: e.dma_start(out=vrow[0:24, :], in_=conv_a_w.rearrange("k (c p) -> (k c) p", p=128)), writes=[vrow_b])
    DMA("sp", "c_vrow", lambda e: e.dma_start(out=vrow[24:32, :], in_=pool_scale.rearrange("(c p) -> c p", p=128)), writes=[vrow_b])
    DMA("sp", "c_vrow", lambda e: e.dma_start(out=vrow[32:98, :], in_=ffn_conv_w.rearrange("k (c p) -> (k c) p", p=128)), writes=[vrow_b])
    POOL(lambda e: e.iota(iot[:], pattern=[[1, 128]], base=0, channel_multiplier=-1), writes=[iot_b])
    DVE(lambda e: e.tensor_copy(out=identf[:], in_=iot[:]), reads=[iot_b], writes=[ident_b])
    DVE(lambda e: e.tensor_single_scalar(out=identf[:], in_=identf[:], scalar=0.0, op=ALU.is_equal), reads=[ident_b], writes=[ident_b])
    DVE(lambda e: e.tensor_copy(out=identb[:], in_=identf[:]), reads=[ident_b], writes=[ident_b])
    POOL(lambda e: e.iota(iot[:, 0:16], pattern=[[1, 16]], base=1, channel_multiplier=0), reads=[ident_b], writes=[iot_b])
    DVE(lambda e: e.tensor_copy(out=io16[:], in_=iot[:, 0:16]), reads=[iot_b], writes=[io16_b])
    for g, w in enumerate(POOLW):
        DVE(lambda e, g=g, w=w: e.tensor_scalar(out=rc16[:, g, :], in0=io16[:], scalar1=float(w), scalar2=None, op0=ALU.min),
            reads=[io16_b], writes=[rc16_b])
    DVE(lambda e: e.reciprocal(out=rc16[:], in_=rc16[:]), reads=[rc16_b], writes=[rc16_b])
    def emit_vecT():
        bk = take_banks(1)[0]
        PE(lambda e, bk=bk: e.transpose(out=banks[bk][:, 0:128], in_=vrow[:], identity=identf[:]), reads=[vrow_b, ident_b], writes=[banks_b[bk]])
        DVE(lambda e, bk=bk: e.tensor_copy(out=vecT[:], in_=banks[bk][:, 0:128]), reads=[banks_b[bk]], writes=[vecT_b])
    POOL(lambda e: e.memset(epsT[:], EPS), writes=[epsT_b])
    POOL(lambda e: e.memset(cA_p[:], 0.0), writes=cA_pb)
    POOL(lambda e: e.memset(cP_p[:], 0.0), writes=cP_pb)
    POOL(lambda e: e.memset(cF_p[:], 0.0), writes=cF_pb)

    def sview(slot, nk, ncols):
        return slot[:, 0:nk * ncols].rearrange("p (k n) -> p k n", n=ncols)

    def wsrc(w, k0, nk, c0, ncols):
        return w[k0 * 128:(k0 + nk) * 128, c0:c0 + ncols].rearrange("(k p) n -> p k n", p=128)

    tile_blocks = []
    for j in range(4):
        tile_blocks.append((f"A{j}", [(lambda sl, s=s: sview(sl, 8, 768)[:, :, s * 256:(s + 1) * 256], wsrc(w_in, 0, 8, s * D + 256 * j, 256)) for s in range(3)], 8 * 768))
        if j == 0:
            tile_blocks.append(("U", [(lambda sl: sview(sl, 8, D), wsrc(w_in, 0, 8, 3 * D, D))], 8 * D))

    def gblock(jg):
        return (f"G{jg}", [(lambda sl, s=s: sview(sl, 8, 1024)[:, :, s * 512:(s + 1) * 512], wsrc(w_in, 0, 8, (4 + s) * D + 512 * jg, 512)) for s in range(2)], 8 * 1024)
    tile_blocks.append(gblock(0))
    tile_blocks.append(("AO", [(lambda sl: sview(sl, 8, D), wsrc(w_a_out, 0, 8, 0, D))], 8 * D))
    tile_blocks.append(("PW", [(lambda sl: sview(sl, 8, 256), pool_w.rearrange("g (k p) n -> p (g k) n", p=128))], 8 * 256))
    tile_blocks.append(gblock(1))
    tile_blocks.append(("O", [(lambda sl: sview(sl, 8, D), wsrc(w_o, 0, 8, 0, D))], 8 * D))
    for j in range(6):
        ncol = 512 if j < 5 else 256
        tile_blocks.append((f"UP{j}", [(lambda sl, s=s, ncol=ncol: sview(sl, 8, 1024)[:, :, s * 512:s * 512 + ncol], wsrc(w_up, 0, 8, s * DFF + 512 * j, ncol)) for s in range(2)], 8 * 1024))
    DKB = [(0, 8), (8, 8), (16, 6)]
    for kb, (k0, nk) in enumerate(DKB):
        tile_blocks.append((f"D{kb}", [(lambda sl, nk=nk: sview(sl, nk, D), wsrc(w_down, k0, nk, 0, D))], nk * D))
    tile_blocks.append(("E1", [(lambda sl: sview(sl, 8, D), wsrc(w_ple_gate, 0, 8, 0, D))], 8 * D))
    tile_blocks.append(("E2", [(lambda sl: sview(sl, 2, D), wsrc(w_ple_proj, 0, 2, 0, D))], 2 * D))
    NBLK = len(tile_blocks)
    n_tiles_total = nt_prompt
    NG = NBLK * n_tiles_total
    wscr = nc.dram_tensor("wscratch", [NBLK, 128, SLOT_ELEMS], BF16, kind="Internal").ap()
    wscr_b = [Buf(f"wscr{i}") for i in range(NBLK)]
    ws = {"next_load": 0, "next_acq": 0, "free": list(range(NSLOTS)), "slot_of": {}}

    def emit_load(j, si):
        bi = j % NBLK
        name, dmas, nel = tile_blocks[bi]
        if j < NBLK:
            for dst_fn, src in dmas:
                DMA("pool", f"w_slot{si}", lambda e, dst_fn=dst_fn, src=src, si=si: e.dma_start(out=dst_fn(slots[si]), in_=src), writes=[slots_b[si]])
            if n_tiles_total > 1:
                DMA("sp", f"w_wb{bi}", lambda e, si=si, bi=bi, nel=nel: e.dma_start(out=wscr[bi, :, 0:nel], in_=slots[si][:, 0:nel]), reads=[slots_b[si]], writes=[wscr_b[bi]])
        else:
            DMA("sp", f"w_slot{si}", lambda e, si=si, bi=bi, nel=nel: e.dma_start(out=slots[si][:, 0:nel], in_=wscr[bi, :, 0:nel]), reads=[wscr_b[bi]], writes=[slots_b[si]])

    def pump_loads():
        while ws["next_load"] < NG and ws["free"]:
            si = ws["free"].pop(0)
            ws["slot_of"][ws["next_load"]] = si
            emit_load(ws["next_load"], si)
            ws["next_load"] += 1

    def acquire(name):
        j = ws["next_acq"]
        assert tile_blocks[j % NBLK][0] == name, (tile_blocks[j % NBLK][0], name)
        ws["next_acq"] += 1
        pump_loads()
        assert ws["next_load"] > j, "weight ring deadlock: block %d (%s) not loadable" % (j, name)
        return j, ws["slot_of"][j]

    def release(j):
        ws["free"].append(ws["slot_of"][j])
        pump_loads()

    class Ctx:
        pass

    def make_prompt_ctx(tidx, h=None):
        c = Ctx()
        hs = (0, 1) if h is None else (h,)
        c.kind = "prompt"; c.key = "p" + ("f" if h is None else "ab"[h])
        c.NSUB = 2 * len(hs); c.TOK = 256 * len(hs); c.L = c.TOK; c.nseq = 1; c.R = 128
        col0 = 256 * hs[0]
        sub0 = 2 * hs[0]
        r0 = tidx * 512 + col0
        c.x_src = xp[r0:r0 + c.TOK, :]; c.p_src = pp[r0:r0 + c.TOK, :]; c.y_dst = yp[r0:r0 + c.TOK, :]
        c.xres = lambda s, sub0=sub0: xres[0:128, sub0 + s, :]
        c.xres_b = xres_b[sub0:sub0 + c.NSUB]
        c.xkey = lambda s, sub0=sub0: f"p{sub0 + s}"
        cs = slice(col0, col0 + c.TOK)
        c.actT = actT[:, :, cs]; c.actT_b = actT_b[sub0:sub0 + c.NSUB]
        c.actT_sub = lambda s, sub0=sub0: [actT_b[sub0 + s]]
        c.za = big[:, 0:8, cs]; c.dd = big[:, 8:16, cs]; c.mg = big[:, 16:24, cs]; c.ff = big[:, 0:NFC, cs]
        c.bufs = lambda lo, hi: [b_ for h_ in hs for b_ in big_b[h_][lo:hi]]
        c.buf1 = lambda i: [big_b[h_][i] for h_ in hs]
        c.pT = pT[:, :, cs]; c.pT_b = [pT_b[h_] for h_ in hs]
        c.cA, c.cP, c.cF = cA_p, cP_p, cF_p
        c.cA_b, c.cP_b, c.cF_b = cA_pb, cP_pb, cF_pb
        c.first = (tidx == 0 and hs[0] == 0)
        c.g0 = (tidx == 0)
        return c

    def make_sample_ctx():
        c = Ctx()
        c.kind = "sample"; c.key = "s"
        c.L = DEC_S; c.nseq = SPC; c.R = NS; c.NSUB = 1; c.TOK = NS
        c.x_src = xs; c.p_src = psm; c.y_dst = ys
        c.xres = lambda s: xres_s[0:NS, 0, :]
        c.xres_b = xres_sb
        c.xkey = lambda s: "s"
        c.actT = actT_s[:, :, :]; c.actT_b = actT_sb
        c.actT_sub = lambda s: actT_sb
        c.za = big_s[:, 0:8, :]; c.dd = big_s[:, 8:16, :]; c.mg = big_s[:, 16:24, :]; c.ff = big_s[:, 0:NFC, :]
        c.bufs = lambda lo, hi: big_sb[lo:hi]
        c.buf1 = lambda i: [big_sb[i]]
        c.pT = pT_s[:, :, :]; c.pT_b = [pT_sb]
        c.cA, c.cP, c.cF = cA_s, cP_s, cF_s
        c.cA_b, c.cP_b, c.cF_b = cA_sb, cP_sb, cF_sb
        c.first = False
        c.g0 = True
        return c

    def PO(c):
        return DVE if c.g0 else POOL

    def DQ(c):
        return "pool"

    def ext_view(c, ti, pre):
        n = c.nseq * (pre + c.L)
        return temps[ti][:, 0:n].rearrange("p (s j) -> p s j", j=pre + c.L)

    def tok_view(c, ap2d):
        return ap2d.rearrange("p (s j) -> p s j", j=c.L)

    def carry_view(c, carr, ch, pre):
        return carr[:, ch, :].rearrange("p (s j) -> p s j", j=pre)

    def row_rstd(c, src_ap, src_bufs):
        si = take_stat()
        ACT(lambda e, si=si: e.memzero(stats[:, si:si + 1]), writes=[stats_b[si]])
        ACT(lambda e, si=si: e.activation(out=junk[0:c.R, :], in_=src_ap, func=AF.Square, scale=1.0 / 32.0, accum_out=stats[0:c.R, si:si + 1]),
            reads=list(src_bufs), writes=[junk_b, stats_b[si]])
        ACT(lambda e, si=si: e.activation(out=stats[0:c.R, si:si + 1], in_=stats[0:c.R, si:si + 1], func=AF.Sqrt, bias=epsT[0:c.R, 0:1], scale=1.0),
            reads=[stats_b[si], epsT_b], writes=[stats_b[si]])
        DVE(lambda e, si=si: e.reciprocal(out=stats[0:c.R, si:si + 1], in_=stats[0:c.R, si:si + 1]), reads=[stats_b[si]], writes=[stats_b[si]])
        return si

    def norm_to_bf16(c, s, gidx, src=None, src_b=None):
        src = c.xres(s) if src is None else src
        src_b = [c.xres_b[s]] if src_b is None else src_b
        si = row_rstd(c, src, src_b)
        xi = xb_rr[0]
        xb_rr[0] = (xi + 1) % 2
        DVE(lambda e, si=si, xi=xi: e.scalar_tensor_tensor(out=xb[xi][0:c.R, :], in0=src, scalar=stats[0:c.R, si:si + 1],
                                                             in1=gbc[0:c.R, gidx, :], op0=ALU.mult, op1=ALU.mult),
            reads=list(src_b) + [stats_b[si], gbc_b[gidx]], writes=[xb_b[xi]])
        return xi

    def transposes_sub(c, s, src_ap, src_bufs, nchunks, dst, dst_bufs, evac_dve=False, defer_evac=False):
        bk = take_banks(1)[0]
        pv = banks[bk][:].bitcast(BF16)

        def fn(e):
            ins = None
            for ch in range(nchunks):
                ins = e.transpose(out=pv[:, ch * 128:ch * 128 + c.R], in_=src_ap[:, ch * 128:(ch + 1) * 128], identity=identb[0:c.R, 0:c.R])
            return ins
        PE(fn, reads=list(src_bufs) + [ident_b], writes=[banks_b[bk]])
        pv3 = pv[:, 0:nchunks * 128].rearrange("p (k t) -> p k t", t=128)
        def evac():
            if evac_dve:
                DVE(lambda e: e.tensor_copy(out=dst[:, 0:nchunks, s * 128:s * 128 + c.R], in_=pv3[:, :, 0:c.R]), reads=[banks_b[bk]], writes=list(dst_bufs))
            else:
                ACT(lambda e: e.copy(out=dst[:, 0:nchunks, s * 128:s * 128 + c.R], in_=pv3[:, :, 0:c.R]), reads=[banks_b[bk]], writes=list(dst_bufs))
        if defer_evac:
            return evac
        evac()
        return None

    stg_rr = [0]

    pre_state = {}

    def state_dma(src, name, r0, rn, c0, wg_):
        bi_ = stg_rr[0]; stg_rr[0] = 1 - bi_
        stg = tns[bi_]; stg_b = tns_b[bi_]
        DMA("pool", f"st_in{bi_}", lambda e, r0=r0, rn=rn, c0=c0, wg_=wg_, stg=stg: e.dma_start(out=stg[0:rn, 0:wg_], in_=src[r0:r0 + rn, c0:c0 + wg_]),
            writes=[stg_b])
        return stg, stg_b

    def load_state(src, nrows, width, carr, carr_b, name=None):
        rt = [(0, min(128, nrows))] + ([(128, nrows - 128)] if nrows > 128 else [])
        for c0 in range(0, width, D):
            wg_ = min(D, width - c0)
            for (r0, rn) in rt:
                if (name, r0, c0) in pre_state:
                    stg, stg_b = pre_state.pop((name, r0, c0))
                else:
                    stg, stg_b = state_dma(src, name, r0, rn, c0, wg_)
                for ci in range(wg_ // 128):
                    ch = c0 // 128 + ci
                    bk = take_banks(1)[0]
                    PE(lambda e, ci=ci, bk=bk, rn=rn, stg=stg: e.transpose(out=banks[bk][:, 0:rn], in_=stg[0:rn, ci * 128:(ci + 1) * 128], identity=identf[0:rn, 0:rn]),
                       reads=[stg_b, ident_b], writes=[banks_b[bk]])
                    DVE(lambda e, ch=ch, bk=bk, r0=r0, rn=rn: e.tensor_copy(out=carr[:, ch, r0:r0 + rn], in_=banks[bk][:, 0:rn]),
                        reads=[banks_b[bk]], writes=[carr_b[ch]])

    def s_pre(c, what):
        if c.kind != "sample":
            return
        if what == "1A":
            load_state(sca, SPC * 2, D, cA_s, cA_sb, name="sca")
        elif what == "1P":
            load_state(spl, SPC * 15, D, cP_s, cP_sb, name="spl")
        elif what == "3":
            load_state(sff, SPC * 2, DFF, cF_s, cF_sb, name="sff")

    def s_norm_T_sub(c, s, gidx, defer_evac=False):
        xi = norm_to_bf16(c, s, gidx)
        ev = transposes_sub(c, s, xb[xi][0:c.R, :], [xb_b[xi]], 8, c.actT, c.actT_sub(s), defer_evac=defer_evac)
        return [ev] if ev is not None else []

    def s_load_sub(c, s):
        R = c.R
        DMA(DQ(c), f"x_in_{c.xkey(s)}", lambda e, s=s: e.dma_start(out=c.xres(s), in_=c.x_src[s * R:(s + 1) * R, :]), writes=[c.xres_b[s]])

    def s_1A(c, j, wv, wbuf):
        L, TOK = c.L, c.TOK
        for cc in range(2):
            ch = 2 * j + cc
            bks = take_banks(3)

            def fn(e, cc=cc, bks=bks):
                ins = None
                for s in range(3):
                    for k in range(8):
                        ins = e.matmul(banks[bks[s]][:, 0:TOK], lhsT=wv[:, k, s * 256 + cc * 128:s * 256 + (cc + 1) * 128], rhs=c.actT[:, k, :],
                                       start=(k == 0), stop=(k == 7))
                return ins
            PE(fn, reads=[wbuf] + list(c.actT_b), writes=[banks_b[b] for b in bks])
            bB, bC, bH = bks
            t_c = take_tmp(); t_e = take_tmp(); t_v = take_tmp()
            ACT(lambda e, t_c=t_c, bC=bC: e.copy(out=temps[t_c][:, 0:TOK], in_=banks[bC][:, 0:TOK]), reads=[banks_b[bC]], writes=[temps_b[t_c]])
            ev = ext_view(c, t_e, 2)
            PO(c)(lambda e, ev=ev, ch=ch: e.tensor_copy(out=ev[:, :, 0:2], in_=carry_view(c, c.cA, ch, 2)), reads=[c.cA_b[ch]], writes=[temps_b[t_e]])
            DVE(lambda e, ev=ev, bH=bH, t_c=t_c: e.tensor_tensor(out=ev[:, :, 2:2 + L], in0=tok_view(c, banks[bH][:, 0:TOK]),
                                                                 in1=tok_view(c, temps[t_c][:, 0:TOK]), op=ALU.mult),
                reads=[banks_b[bH], temps_b[t_c]], writes=[temps_b[t_e]])
            PO(c)(lambda e, ev=ev, ch=ch: e.tensor_copy(out=carry_view(c, c.cA, ch, 2), in_=ev[:, :, L:L + 2]), reads=[temps_b[t_e]], writes=[c.cA_b[ch]])
            vv = tok_view(c, temps[t_v][:, 0:TOK])
            ACT(lambda e, ev=ev, vv=vv, ch=ch: e.activation(out=vv, in_=ev[:, :, 0:L], func=AF.Copy, scale=vecT[:, CAW0 + ch:CAW0 + ch + 1]),
                reads=[temps_b[t_e], vecT_b], writes=[temps_b[t_v]])
            for k in (1, 2):
                DVE(lambda e, ev=ev, vv=vv, ch=ch, k=k: e.scalar_tensor_tensor(out=vv, in0=ev[:, :, k:k + L], scalar=vecT[:, CAW0 + k * 8 + ch:CAW0 + k * 8 + ch + 1],
                                                                                 in1=vv, op0=ALU.mult, op1=ALU.add),
                    reads=[temps_b[t_e], temps_b[t_v], vecT_b], writes=[temps_b[t_v]])
            DVE(lambda e, ch=ch, bB=bB, t_v=t_v: e.tensor_tensor(out=c.za[:, ch, :], in0=banks[bB][:, 0:TOK], in1=temps[t_v][:, 0:TOK], op=ALU.mult),
                reads=[banks_b[bB], temps_b[t_v]], writes=c.buf1(ch))

    def s_1P(c, wv, wbuf, chs=range(8)):
        L, TOK = c.L, c.TOK
        for ch in chs:
            g = ch // 2
            w = POOLW[g]
            bk = take_banks(1)[0]

            def fn(e, ch=ch, bk=bk):
                ins = None
                for k in range(8):
                    ins = e.matmul(banks[bk][:, 0:TOK], lhsT=wv[:, k, ch * 128:(ch + 1) * 128], rhs=c.actT[:, k, :], start=(k == 0), stop=(k == 7))
                return ins
            PE(fn, reads=[wbuf] + list(c.actT_b), writes=[banks_b[bk]])
            t_u = take_tmp()
            uv = ext_view(c, t_u, 15)
            PO(c)(lambda e, uv=uv, ch=ch: e.tensor_copy(out=uv[:, :, 0:15], in_=carry_view(c, c.cP, ch, 15)), reads=[c.cP_b[ch]], writes=[temps_b[t_u]])
            ACT(lambda e, uv=uv, bk=bk: e.copy(out=uv[:, :, 15:15 + L], in_=tok_view(c, banks[bk][:, 0:TOK])), reads=[banks_b[bk]], writes=[temps_b[t_u]])
            PO(c)(lambda e, uv=uv, ch=ch: e.tensor_copy(out=carry_view(c, c.cP, ch, 15), in_=uv[:, :, L:L + 15]), reads=[temps_b[t_u]], writes=[c.cP_b[ch]])
            prev = uv; prev_t = t_u
            sh = 1
            while sh < w:
                t_s = take_tmp()
                sv = ext_view(c, t_s, 15)
                lo = 2 * sh - 1
                (PO(c) if sh == 1 else DVE)(lambda e, sv=sv, prev=prev, lo=lo, sh=sh: e.tensor_tensor(out=sv[:, :, lo:15 + L], in0=prev[:, :, lo:15 + L],
                                                                                                       in1=prev[:, :, lo - sh:15 + L - sh], op=ALU.add),
                                           reads=[temps_b[prev_t]], writes=[temps_b[t_s]])
                prev = sv; prev_t = t_s
                sh *= 2
            DVE(lambda e, prev=prev, uv=uv, ch=ch, w=w: e.scalar_tensor_tensor(out=tok_view(c, c.dd[:, ch, :]), in0=prev[:, :, 15:15 + L], scalar=1.0 / w,
                                                                                in1=uv[:, :, 15:15 + L], op0=ALU.mult, op1=ALU.subtract),
                reads=[temps_b[prev_t], temps_b[t_u]], writes=c.buf1(8 + ch))
            if c.first:
                t_f = take_tmp()
                DVE(lambda e, prev=prev, g=g, t_f=t_f: e.tensor_tensor(out=temps[t_f][:, 0:16], in0=prev[:, 0, 15:31], in1=rc16[:, g, :], op=ALU.mult),
                    reads=[temps_b[prev_t], rc16_b], writes=[temps_b[t_f]])
                DVE(lambda e, uv=uv, ch=ch, t_f=t_f: e.tensor_tensor(out=c.dd[:, ch, 0:16], in0=temps[t_f][:, 0:16], in1=uv[:, 0, 15:31], op=ALU.subtract),
                    reads=[temps_b[t_f], temps_b[t_u]], writes=c.buf1(8 + ch))

    def s_1G(c, jg, wg, wa, wp, wbufs):
        TOK = c.TOK
        for cc in range(4):
            ch = 4 * jg + cc
            g = ch // 2
            bks = take_banks(4)

            def fn(e, cc=cc, ch=ch, g=g, bks=bks):
                ins = None
                for s in range(2):
                    for k in range(8):
                        ins = e.matmul(banks[bks[s]][:, 0:TOK], lhsT=wg[:, k, s * 512 + cc * 128:s * 512 + (cc + 1) * 128], rhs=c.actT[:, k, :],
                                       start=(k == 0), stop=(k == 7))
                for k in range(8):
                    ins = e.matmul(banks[bks[2]][:, 0:TOK], lhsT=wa[:, k, ch * 128:(ch + 1) * 128], rhs=c.za[:, k, :], start=(k == 0), stop=(k == 7))
                for k in range(2):
                    ins = e.matmul(banks[bks[3]][:, 0:TOK], lhsT=wp[:, g * 2 + k, (ch % 2) * 128:(ch % 2 + 1) * 128], rhs=c.dd[:, g * 2 + k, :],
                                   start=(k == 0), stop=(k == 1))
                return ins
            PE(fn, reads=list(wbufs) + list(c.actT_b) + c.bufs(0, 16), writes=[banks_b[b] for b in bks])
            t_a = take_tmp(); t_p = take_tmp()
            ACT(lambda e, t_a=t_a, b=bks[0]: e.activation(out=temps[t_a][:, 0:TOK], in_=banks[b][:, 0:TOK], func=AF.Sigmoid), reads=[banks_b[bks[0]]], writes=[temps_b[t_a]])
            ACT(lambda e, t_p=t_p, b=bks[1]: e.activation(out=temps[t_p][:, 0:TOK], in_=banks[b][:, 0:TOK], func=AF.Sigmoid), reads=[banks_b[bks[1]]], writes=[temps_b[t_p]])
            DVE(lambda e, t_a=t_a, b=bks[2]: e.tensor_tensor(out=temps[t_a][:, 0:TOK], in0=banks[b][:, 0:TOK], in1=temps[t_a][:, 0:TOK], op=ALU.mult),
                reads=[banks_b[bks[2]], temps_b[t_a]], writes=[temps_b[t_a]])
            DVE(lambda e, t_p=t_p, b=bks[3], ch=ch: e.scalar_tensor_tensor(out=temps[t_p][:, 0:TOK], in0=banks[b][:, 0:TOK], scalar=vecT[:, PSC0 + ch:PSC0 + ch + 1],
                                                                             in1=temps[t_p][:, 0:TOK], op0=ALU.mult, op1=ALU.mult),
                reads=[banks_b[bks[3]], temps_b[t_p], vecT_b], writes=[temps_b[t_p]])
            DVE(lambda e, t_a=t_a, t_p=t_p, ch=ch: e.tensor_tensor(out=c.mg[:, ch, :], in0=temps[t_a][:, 0:TOK], in1=temps[t_p][:, 0:TOK], op=ALU.add),
                reads=[temps_b[t_a], temps_b[t_p]], writes=c.buf1(16 + ch))

    def s_tm(c, lhs, lhs_b, nk, wv_fn, w_bufs, gidx, post=None, lag=1):
        R = c.R
        for s in range(c.NSUB + lag):
            if s >= c.NSUB:
                if post is not None and s - lag >= 0:
                    for ev in (post(s - lag) or []):
                        ev()
                continue
            bks = take_banks(2)

            def fn(e, s=s, bks=bks):
                ins = None
                for hf in range(2):
                    for k in range(nk):
                        ins = e.matmul(banks[bks[hf]][0:R, :], lhsT=lhs[:, k, s * 128:s * 128 + R], rhs=wv_fn(k, hf), start=(k == 0), stop=(k == nk - 1))
                return ins
            PE(fn, reads=list(w_bufs) + list(lhs_b), writes=[banks_b[b] for b in bks])
            late = []
            if s >= lag and post is not None:
                late = post(s - lag) or []
            sa = take_stat(); sbb = take_stat()
            ti_ = tn_rr[0]; tn_rr[0] = 1 - ti_
            tn = tns[ti_]; tn_b = tns_b[ti_]
            ACT(lambda e, sa=sa: e.memzero(stats[:, sa:sa + 1]), writes=[stats_b[sa]])
            ACT(lambda e, sbb=sbb: e.memzero(stats[:, sbb:sbb + 1]), writes=[stats_b[sbb]])
            ACT(lambda e, sa=sa, b=bks[0]: e.activation(out=junk[0:R, 0:512], in_=banks[b][0:R, :], func=AF.Square, scale=1.0 / 32.0, accum_out=stats[0:R, sa:sa + 1]),
                reads=[banks_b[bks[0]]], writes=[junk_b, stats_b[sa]])
            DVE(lambda e, b=bks[1], tn=tn: e.tensor_tensor(out=tn[0:R, 512:1024], in0=banks[b][0:R, :], in1=gbc[0:R, gidx, 512:1024], op=ALU.mult),
                reads=[banks_b[bks[1]], gbc_b[gidx]], writes=[tn_b])
            ACT(lambda e, sbb=sbb, b=bks[1]: e.activation(out=junk[0:R, 512:1024], in_=banks[b][0:R, :], func=AF.Square, scale=1.0 / 32.0, accum_out=stats[0:R, sbb:sbb + 1]),
                reads=[banks_b[bks[1]]], writes=[junk_b, stats_b[sbb]])
            DVE(lambda e, b=bks[0], tn=tn: e.tensor_tensor(out=tn[0:R, 0:512], in0=banks[b][0:R, :], in1=gbc[0:R, gidx, 0:512], op=ALU.mult),
                reads=[banks_b[bks[0]], gbc_b[gidx]], writes=[tn_b])
            DVE(lambda e, sa=sa, sbb=sbb: e.tensor_tensor(out=stats[0:R, sa:sa + 1], in0=stats[0:R, sa:sa + 1], in1=stats[0:R, sbb:sbb + 1], op=ALU.add),
                reads=[stats_b[sa], stats_b[sbb]], writes=[stats_b[sa]])
            ACT(lambda e, sa=sa: e.activation(out=stats[0:R, sa:sa + 1], in_=stats[0:R, sa:sa + 1], func=AF.Sqrt, bias=epsT[0:R, 0:1], scale=1.0),
                reads=[stats_b[sa], epsT_b], writes=[stats_b[sa]])
            DVE(lambda e, sa=sa: e.reciprocal(out=stats[0:R, sa:sa + 1], in_=stats[0:R, sa:sa + 1]), reads=[stats_b[sa]], writes=[stats_b[sa]])
            DVE(lambda e, s=s, sa=sa, tn=tn: e.scalar_tensor_tensor(out=c.xres(s), in0=tn[0:R, :], scalar=stats[0:R, sa:sa + 1], in1=c.xres(s),
                                                                     op0=ALU.mult, op1=ALU.add),
                reads=[c.xres_b[s], tn_b, stats_b[sa]], writes=[c.xres_b[s]])
            for ev in late:
                ev()

    def s_3_sample(c, j, nch, wu, wbuf):
        L, TOK = c.L, c.TOK
        ch0 = 4 * j
        W = nch * TOK
        bA, bG = take_banks(2)

        def fn(e):
            ins = None
            for s, bk in ((0, bA), (1, bG)):
                for cc in range(nch):
                    for k in range(8):
                        ins = e.matmul(banks[bk][:, cc * TOK:(cc + 1) * TOK], lhsT=wu[:, k, s * 512 + cc * 128:s * 512 + (cc + 1) * 128], rhs=c.actT[:, k, :],
                                       start=(k == 0), stop=(k == 7))
            return ins
        PE(fn, reads=[wbuf] + list(c.actT_b), writes=[banks_b[bA], banks_b[bG]])
        t_e = take_tmp(); t_v = take_tmp()
        n6 = c.nseq * (L + 2)
        ev4 = temps[t_e][:, 0:nch * n6].rearrange("p (c s j) -> p c s j", c=nch, j=L + 2)
        cfv = c.cF[:, ch0:ch0 + nch, :].rearrange("p c (s j) -> p c s j", j=2)
        cbufs = list(c.cF_b[ch0:ch0 + nch])
        PO(c)(lambda e: e.tensor_copy(out=ev4[:, :, :, 0:2], in_=cfv), reads=cbufs, writes=[temps_b[t_e]])
        ACT(lambda e: e.copy(out=ev4[:, :, :, 2:2 + L], in_=banks[bA][:, 0:W].rearrange("p (c s j) -> p c s j", c=nch, j=L)), reads=[banks_b[bA]], writes=[temps_b[t_e]])
        PO(c)(lambda e: e.tensor_copy(out=cfv, in_=ev4[:, :, :, L:L + 2]), reads=[temps_b[t_e]], writes=cbufs)
        for cc in range(nch):
            ch = ch0 + cc
            ev = ev4[:, cc, :, :]
            vv = temps[t_v][:, cc * TOK:(cc + 1) * TOK].rearrange("p (s j) -> p s j", j=L)
            ACT(lambda e, ev=ev, vv=vv, ch=ch: e.activation(out=vv, in_=ev[:, :, 0:L], func=AF.Copy, scale=vecT[:, FCW0 + ch:FCW0 + ch + 1]),
                reads=[temps_b[t_e], vecT_b], writes=[temps_b[t_v]])
            for k in (1, 2):
                DVE(lambda e, ev=ev, vv=vv, ch=ch, k=k: e.scalar_tensor_tensor(out=vv, in0=ev[:, :, k:k + L], scalar=vecT[:, FCW0 + k * NFC + ch:FCW0 + k * NFC + ch + 1],
                                                                                 in1=vv, op0=ALU.mult, op1=ALU.add),
                    reads=[temps_b[t_e], temps_b[t_v], vecT_b], writes=[temps_b[t_v]])
        ACT(lambda e: e.activation(out=temps[t_v][:, 0:W], in_=temps[t_v][:, 0:W], func=AF.Gelu_apprx_tanh), reads=[temps_b[t_v]], writes=[temps_b[t_v]])
        DVE(lambda e: e.tensor_tensor(out=c.ff[:, ch0:ch0 + nch, :], in0=banks[bG][:, 0:W].rearrange("p (c t) -> p c t", c=nch),
                                      in1=temps[t_v][:, 0:W].rearrange("p (c t) -> p c t", c=nch), op=ALU.mult),
            reads=[banks_b[bG], temps_b[t_v]], writes=c.bufs(ch0, ch0 + nch))

    def s_3(c, j, nch, wu, wbuf):
        if c.kind == "sample":
            return s_3_sample(c, j, nch, wu, wbuf)
        L, TOK = c.L, c.TOK
        pend = c.__dict__.setdefault("s3_pend", [])
        for cc in range(nch):
            ch = 4 * j + cc
            bks = take_banks(2)

            def fn(e, cc=cc, bks=bks):
                ins = None
                for s in range(2):
                    for k in range(8):
                        ins = e.matmul(banks[bks[s]][:, 0:TOK], lhsT=wu[:, k, s * 512 + cc * 128:s * 512 + (cc + 1) * 128], rhs=c.actT[:, k, :],
                                       start=(k == 0), stop=(k == 7))
                return ins
            PE(fn, reads=[wbuf] + list(c.actT_b), writes=[banks_b[b] for b in bks])
            bA, bG = bks
            t_e = take_tmp(); t_v = take_tmp()
            ev = ext_view(c, t_e, 2)
            PO(c)(lambda e, ev=ev, ch=ch: e.tensor_copy(out=ev[:, :, 0:2], in_=carry_view(c, c.cF, ch, 2)), reads=[c.cF_b[ch]], writes=[temps_b[t_e]])
            ACT(lambda e, ev=ev, bA=bA: e.copy(out=ev[:, :, 2:2 + L], in_=tok_view(c, banks[bA][:, 0:TOK])), reads=[banks_b[bA]], writes=[temps_b[t_e]])
            PO(c)(lambda e, ev=ev, ch=ch: e.tensor_copy(out=carry_view(c, c.cF, ch, 2), in_=ev[:, :, L:L + 2]), reads=[temps_b[t_e]], writes=[c.cF_b[ch]])
            vv = tok_view(c, temps[t_v][:, 0:TOK])
            ACT(lambda e, ev=ev, vv=vv, ch=ch: e.activation(out=vv, in_=ev[:, :, 0:L], func=AF.Copy, scale=vecT[:, FCW0 + ch:FCW0 + ch + 1]),
                reads=[temps_b[t_e], vecT_b], writes=[temps_b[t_v]])
            for k in (1, 2):
                DVE(lambda e, ev=ev, vv=vv, ch=ch, k=k: e.scalar_tensor_tensor(out=vv, in0=ev[:, :, k:k + L], scalar=vecT[:, FCW0 + k * NFC + ch:FCW0 + k * NFC + ch + 1],
                                                                                 in1=vv, op0=ALU.mult, op1=ALU.add),
                    reads=[temps_b[t_e], temps_b[t_v], vecT_b], writes=[temps_b[t_v]])
            s_3_flush(c)
            pend.append((ch, bG, t_v))
        s_3_flush(c)

    def s_3_flush(c):
        TOK = c.TOK
        pend = c.__dict__.setdefault("s3_pend", [])
        while pend:
            ch, bG, t_v = pend.pop(0)
            ACT(lambda e, t_v=t_v: e.activation(out=temps[t_v][:, 0:TOK], in_=temps[t_v][:, 0:TOK], func=AF.Gelu_apprx_tanh), reads=[temps_b[t_v]], writes=[temps_b[t_v]])
            DVE(lambda e, ch=ch, bG=bG, t_v=t_v: e.tensor_tensor(out=c.ff[:, ch, :], in0=banks[bG][:, 0:TOK], in1=temps[t_v][:, 0:TOK], op=ALU.mult),
                reads=[banks_b[bG], temps_b[t_v]], writes=c.buf1(ch))

    def s_5pre_sub(c, s):
        R = c.R
        if True:
            xi = xb_rr[0]
            xb_rr[0] = (xi + 1) % 2
            ACT(lambda e, s=s, xi=xi: e.copy(out=xb[xi][0:R, :], in_=c.xres(s)), reads=[c.xres_b[s]], writes=[xb_b[xi]])
            ev1 = transposes_sub(c, s, xb[xi][0:R, :], [xb_b[xi]], 8, c.actT, c.actT_sub(s), defer_evac=True)
            DMA(DQ(c), "p_in", lambda e, s=s: e.dma_start(out=pf[0:R, :], in_=c.p_src[s * R:(s + 1) * R, :]), writes=[pf_b])
            DVE(lambda e: e.tensor_copy(out=pb[0:R, :], in_=pf[0:R, :]), reads=[pf_b], writes=[pb_b])
            ev2 = transposes_sub(c, s, pb[0:R, :], [pb_b], 2, c.pT, c.pT_b, defer_evac=True)
            return [ev1, ev2]

    def s_5(c, wgt, wpr, wbuf, post_a=None, post_b=None):
        R = c.R
        for s in range(c.NSUB):
            grp = []
            for hf in range(2):
                bks = take_banks(2)

                def fn(e, s=s, hf=hf, bks=bks):
                    ins = None
                    for k in range(8):
                        ins = e.matmul(banks[bks[0]][0:R, :], lhsT=c.actT[:, k, s * 128:s * 128 + R], rhs=wgt[:, k, hf * 512:(hf + 1) * 512], start=(k == 0), stop=(k == 7))
                    for k in range(2):
                        ins = e.matmul(banks[bks[1]][0:R, :], lhsT=c.pT[:, k, s * 128:s * 128 + R], rhs=wpr[:, k, hf * 512:(hf + 1) * 512], start=(k == 0), stop=(k == 1))
                    return ins
                PE(fn, reads=list(wbuf) + list(c.pT_b) + c.actT_sub(s), writes=[banks_b[b] for b in bks])
                grp.append(bks)
            if post_a is not None:
                if s >= 1:
                    post_a(s - 1)
                if s == c.NSUB - 1:
                    post_a(s)
            yo = tn_rr[0]; tn_rr[0] = 1 - yo
            for hf in range(2):
                bks = grp[hf]
                t_g = sg_rr[0]; sg_rr[0] = 1 - t_g
                ACT(lambda e, t_g=t_g, b=bks[0]: e.activation(out=sg5[t_g][0:R, :], in_=banks[b][0:R, :], func=AF.Sigmoid), reads=[banks_b[bks[0]]], writes=[sg5_b[t_g]])
                DVE(lambda e, t_g=t_g, b=bks[1]: e.tensor_tensor(out=sg5[t_g][0:R, :], in0=banks[b][0:R, :], in1=sg5[t_g][0:R, :], op=ALU.mult),
                    reads=[banks_b[bks[1]], sg5_b[t_g]], writes=[sg5_b[t_g]])
                PO(c)(lambda e, s=s, hf=hf, t_g=t_g, yo=yo: e.tensor_tensor(out=tns[yo][0:R, hf * 512:(hf + 1) * 512], in0=c.xres(s)[:, hf * 512:(hf + 1) * 512],
                                                                           in1=sg5[t_g][0:R, :], op=ALU.add),
                      reads=[c.xres_b[s], sg5_b[t_g]], writes=[tns_b[yo]])
            DMA(DQ(c), f"y_out_{yo}", lambda e, s=s, yo=yo: e.dma_start(out=c.y_dst[s * R:(s + 1) * R, :], in_=tns[yo][0:R, :]), reads=[tns_b[yo]], is_output=True)
            if post_b is not None:
                post_b(s)

    preloaded = set()

    def prologue_sub(c, s):
        if (c.key, s) not in preloaded:
            s_load_sub(c, s)
        s_norm_T_sub(c, s, 0)

    def stage_loads(n):
        for s in range(n.NSUB):
            DMA(DQ(n), f"x_in_{n.xkey(s)}", lambda e, s=s: e.dma_start(out=xst(s), in_=n.x_src[s * 128:(s + 1) * 128, :]), writes=xst_b(s))

    def prologue_staged_a(n, s):
        xi = norm_to_bf16(n, s, 0, src=xst(s), src_b=xst_b(s))
        transposes_sub(n, s, xb[xi][0:n.R, :], [xb_b[xi]], 8, n.actT, n.actT_sub(s), evac_dve=True)

    def prologue_staged_b(n, s):
        DVE(lambda e, s=s: e.tensor_copy(out=n.xres(s), in_=xst(s)), reads=xst_b(s), writes=[n.xres_b[s]])

    def emit_group(ctxs, nxt):
        stage[0] = "0"
        for c in ctxs:
            if c.kind == "sample" or c.first:
                pend_ev = []
                for s in range(c.NSUB):
                    if (c.key, s) not in preloaded:
                        s_load_sub(c, s)
                    evs = s_norm_T_sub(c, s, 0, defer_evac=True)
                    for ev in pend_ev:
                        ev()
                    pend_ev = evs
                for ev in pend_ev:
                    ev()
                if c.first:
                    emit_vecT()
        for j in range(4):
            stage[0] = f"1A{j}"
            gj, si = acquire(f"A{j}")
            if j == 0:
                gj_u, s_uu = acquire("U")
                wvu = sview(slots[s_uu], 8, D)
            wv = sview(slots[si], 8, 768)
            for c in ctxs:
                if j == 0:
                    s_pre(c, "1A")
                s_1A(c, j, wv, slots_b[si])
            release(gj)
            for c in ctxs:
                if j == 0:
                    s_pre(c, "1P")
                s_1P(c, wvu, slots_b[s_uu], chs=range(2 * j, 2 * j + 2))
        release(gj_u)
        stage[0] = "1G"
        gj_g, s_g = acquire("G0")
        gj_ap, s_ap = acquire("AO")
        gj_pw, s_pw = acquire("PW")
        wa = sview(slots[s_ap], 8, D)
        wp = sview(slots[s_pw], 8, 256)
        for jg in range(2):
            if jg == 1:
                release(gj_g)
                gj_g, s_g = acquire("G1")
            wg = sview(slots[s_g], 8, 1024)
            for c in ctxs:
                s_1G(c, jg, wg, wa, wp, [slots_b[s_g], slots_b[s_ap], slots_b[s_pw]])
        release(gj_g)
        release(gj_ap)
        release(gj_pw)
        stage[0] = "2"
        gj, s_o = acquire("O")
        wo = sview(slots[s_o], 8, D)
        for c in ctxs:
            s_tm(c, c.mg, c.bufs(16, 24), 8, lambda k, hf: wo[:, k, hf * 512:(hf + 1) * 512], [slots_b[s_o]], 1,
                 post=lambda s, c=c: s_norm_T_sub(c, s, 2, defer_evac=True), lag=2)
        release(gj)
        for c in ctxs:
            s_pre(c, "3")
        for j in range(6):
            nch = 4 if j < 5 else 2
            stage[0] = f"3_{j}"
            gj, s_u = acquire(f"UP{j}")
            wu = sview(slots[s_u], 8, 1024)
            for c in ctxs:
                s_3(c, j, nch, wu, slots_b[s_u])
            release(gj)
        for c in ctxs:
            s_3_flush(c)
        stage[0] = "4"
        gd = [acquire(f"D{kb}") for kb in range(3)]
        wd = [sview(slots[gd[kb][1]], DKB[kb][1], D) for kb in range(3)]
        for c in ctxs:
            s_tm(c, c.ff, c.bufs(0, NFC), NFC, lambda k, hf: wd[k // 8][:, k % 8, hf * 512:(hf + 1) * 512], [slots_b[g_[1]] for g_ in gd], 3,
                 post=lambda s, c=c: s_5pre_sub(c, s), lag=1)
        for g_ in gd:
            release(g_[0])
        stage[0] = "5"
        gj_e, s_e = acquire("E1")
        gj_e2, s_e2 = acquire("E2")
        wgt = sview(slots[s_e], 8, D)
        wpr = sview(slots[s_e2], 2, D)
        for c in ctxs:
            if c.kind == "prompt" and nxt is not None:
                stage_loads(nxt)
                s_5(c, wgt, wpr, [slots_b[s_e], slots_b[s_e2]], post_a=lambda s: prologue_staged_a(nxt, s), post_b=lambda s: prologue_staged_b(nxt, s))
            else:
                s_5(c, wgt, wpr, [slots_b[s_e], slots_b[s_e2]])
        release(gj_e)
        release(gj_e2)

    def emit_state_out(carr, carr_b, nch, nrows, dst):
        rts = [(0, min(128, nrows))] + ([(128, nrows - 128)] if nrows > 128 else [])
        for (r0, rn) in rts:
            for g0 in range(0, nch, 2):
                gn = min(2, nch - g0)
                bk = take_banks(1)[0]

                def fn(e, g0=g0, gn=gn, bk=bk, r0=r0, rn=rn):
                    ins = None
                    for i in range(gn):
                        ins = e.transpose(out=banks[bk][0:rn, i * 128:(i + 1) * 128], in_=carr[:, g0 + i, r0:r0 + rn], identity=identf[:])
                    return ins
                PE(fn, reads=[carr_b[g0 + i] for i in range(gn)] + [ident_b], writes=[banks_b[bk]])
                ti = take_tmp()
                DVE(lambda e, bk=bk, ti=ti, gn=gn, rn=rn: e.tensor_copy(out=temps[ti][0:rn, 0:gn * 128], in_=banks[bk][0:rn, 0:gn * 128]), reads=[banks_b[bk]], writes=[temps_b[ti]])
                DMA("pool", "st_out", lambda e, ti=ti, g0=g0, gn=gn, r0=r0, rn=rn: e.dma_start(out=dst[r0:r0 + rn, g0 * 128:(g0 + gn) * 128], in_=temps[ti][0:rn, 0:gn * 128]),
                    reads=[temps_b[ti]], is_output=True)

    pctx = [make_prompt_ctx(t) for t in range(nt_prompt)]
    sctx = make_sample_ctx() if with_sample else None
    for c0 in [pctx[0]] + ([sctx] if with_sample else []):
        for s in range(c0.NSUB):
            s_load_sub(c0, s)
            preloaded.add((c0.key, s))
    if with_sample:
        pre_state[("sca", 0, 0)] = state_dma(sca, "sca", 0, SPC * 2, 0, D)
        pre_state[("spl", 0, 0)] = state_dma(spl, "spl", 0, 128, 0, D)
    pump_loads()
    for t in range(nt_prompt):
        ctxs = [pctx[t]]
        if t == 0 and with_sample:
            ctxs.append(sctx)
        emit_group(ctxs, pctx[t + 1] if t + 1 < nt_prompt else None)
        if t == 0 and with_sample:
            emit_state_out(cA_s, cA_sb, 8, SPC * 2, ncs)
            emit_state_out(cP_s, cP_sb, 8, SPC * 15, nps)
            emit_state_out(cF_s, cF_sb, NFC, SPC * 2, nfs)
    emit_state_out(cA_p, cA_pb, 8, 2, ncp)
    emit_state_out(cP_p, cP_pb, 8, 15, npp)
    emit_state_out(cF_p, cF_pb, NFC, 2, nfp)

    eng_names = {"pe": "tensor", "act": "scalar", "dve": "vector", "pool": "gpsimd", "sp": "sync"}
    sems = {}
    for key in list(T.ENG) + list(T.dma_count.keys()):
        sems[key] = es.enter_context(nc.semaphore(f"s_{key}"))
    final = {}
    for tok in T.out_dma:
        final[tok[0]] = max(final.get(tok[0], 0), tok[1])

    block = es.enter_context(nc.Block())

    class _Cnt:
        def __init__(self, e):
            self._e = e; self.n = 0

        def matmul(self, *a, **k):
            self.n += 1
            return self._e.matmul(*a, **k)

        def transpose(self, *a, **k):
            self.n += 1
            return self._e.transpose(*a, **k)

    pe_counts = []

    def make_stream(ename):
        def body(e):
            for waits, fn, inc in T.streams[ename]:
                for key, val in waits:
                    e.wait_ge(sems[key], val)
                if ename == "pe":
                    ce = _Cnt(e)
                    ins = fn(ce)
                    pe_counts.append(ce.n)
                else:
                    ins = fn(e)
                ins.then_inc(sems[inc[0]], inc[1])
            if ename == "sp":
                for key, val in final.items():
                    e.wait_ge(sems[key], val)
        return body

    for ename in T.ENG:
        getattr(block, eng_names[ename])(make_stream(ename))
    es.close()
    nc._pe_info = list(zip(pe_labels, pe_counts))
    return nc


_CACHE = {}


def _get_nc(nt_prompt, with_sample):
    key = (nt_prompt, with_sample)
    if key not in _CACHE:
        _CACHE[key] = build_program(nt_prompt, with_sample)
    return _CACHE[key]


def make_in_maps(inputs, nt_prompt=4, ncores=NCORES):
    f = lambda a: np.ascontiguousarray(np.asarray(a, dtype=np.float32))
    ntok = nt_prompt * 512
    shared = {
        "g_pre_mix": f(inputs["g_pre_mix"][0]), "w_in": f(inputs["w_in"][0]), "conv_a_w": f(inputs["conv_a_w"][0]),
        "w_a_out": f(inputs["w_a_out"][0]), "pool_w": f(inputs["pool_w"][0]), "pool_scale": f(inputs["pool_scale"][0]),
        "w_o": f(inputs["w_o"][0]), "g_post_mix": f(inputs["g_post_mix"][0]), "g_pre_ffn": f(inputs["g_pre_ffn"][0]),
        "w_up": f(inputs["w_up"][0]), "ffn_conv_w": f(inputs["ffn_conv_w"][0]), "w_down": f(inputs["w_down"][0]),
        "g_post_ffn": f(inputs["g_post_ffn"][0]), "w_ple_proj": f(inputs["w_ple_proj"][0]), "w_ple_gate": f(inputs["w_ple_gate"][0]),
    }
    maps = []
    for b in range(ncores):
        m = dict(shared)
        m["xp"] = f(inputs["x_prompt"][b, :ntok])
        m["pp"] = f(inputs["p_prompt"][0, b, :ntok])
        sl = slice(b * SPC, (b + 1) * SPC)
        m["xs"] = f(inputs["x_sample"][sl]).reshape(SPC * DEC_S, D)
        m["ps"] = f(inputs["p_sample"][0, sl]).reshape(SPC * DEC_S, PLE)
        m["sca"] = f(inputs["state_conv_a"][0, sl]).reshape(SPC * 2, D)
        m["spl"] = f(inputs["state_pool"][0, sl]).reshape(SPC * 15, D)
        m["sff"] = f(inputs["state_ffn_conv"][0, sl]).reshape(SPC * 2, DFF)
        maps.append(m)
    return maps


def kernel(**inputs):
    nc = _get_nc(4, True)
    maps = make_in_maps(inputs)
    res = run_bass_kernel_spmd(nc, maps, core_ids=list(range(NCORES)))
    r = res.results
    y_prompt = np.stack([r[b]["yp"] for b in range(NCORES)], axis=0).astype(np.float32)
    y_sample = np.concatenate([r[b]["ys"].reshape(SPC, DEC_S, D) for b in range(NCORES)], axis=0).astype(np.float32)
    ncp = np.stack([r[b]["ncp"] for b in range(NCORES)], axis=0)[None].astype(np.float32)
    npp = np.stack([r[b]["npp"] for b in range(NCORES)], axis=0)[None].astype(np.float32)
    nfp = np.stack([r[b]["nfp"] for b in range(NCORES)], axis=0)[None].astype(np.float32)
    ncs = np.concatenate([r[b]["ncs"].reshape(SPC, 2, D) for b in range(NCORES)], axis=0)[None].astype(np.float32)
    nps = np.concatenate([r[b]["nps"].reshape(SPC, 15, D) for b in range(NCORES)], axis=0)[None].astype(np.float32)
    nfs = np.concatenate([r[b]["nfs"].reshape(SPC, 2, DFF) for b in range(NCORES)], axis=0)[None].astype(np.float32)
    return (y_prompt, y_sample, ncp, npp, nfp, ncs, nps, nfs)
```

```python
import numpy as np
import concourse.bass as bass
import concourse.mybir as mybir
from concourse.bass_utils import run_bass_kernel_spmd

F32 = mybir.dt.float32
BF16 = mybir.dt.bfloat16
I32 = mybir.dt.int32
AF = mybir.ActivationFunctionType
ALU = mybir.AluOpType

D = 1024
DFF = 2816
NFC = DFF // 128
PLE = 256
EPS = 1e-6
NCORES = 8
SEQ = 2048
DEC_B = 128
DEC_S = 4
SPC = DEC_B // NCORES
POOLW = (2, 4, 8, 16)

SLOT_ELEMS = 8192
NSLOTS = 4
NTEMPS = 12
TEMPW = 528


class Buf:
    __slots__ = ("name", "w", "r", "excl")

    def __init__(self, name, excl=False):
        self.name = name
        self.w = None
        self.r = {}
        self.excl = excl


class Tracker:
    ENG = ("pe", "act", "dve", "pool", "sp")

    def __init__(self):
        self.streams = {e: [] for e in self.ENG}
        self.count = {e: 0 for e in self.ENG}
        self.waited = {e: {} for e in self.ENG}
        self.dma_count = {}
        self.out_dma = []

    def dma_sem(self, name):
        if name not in self.dma_count:
            self.dma_count[name] = 0
        return name

    def emit(self, eng, fn, reads=(), writes=(), dma=None, is_output=False):
        need = {}

        def add(tok, kind):
            if tok is None:
                return
            key, val, teng, isdma = tok
            if not isdma and teng == eng:
                if eng == "pe":
                    return
                if kind in ("WAR", "RAR", "WAW"):
                    return
            if need.get(key, 0) < val:
                need[key] = val

        for b in reads:
            add(b.w, "RAW")
            if b.excl:
                for t in b.r.values():
                    add(t, "RAR")
        for b in writes:
            add(b.w, "WAW")
            for t in b.r.values():
                add(t, "WAR")
        waits = []
        wd = self.waited[eng]
        for key, val in need.items():
            if wd.get(key, 0) >= val:
                continue
            wd[key] = val
            waits.append((key, val))
        if dma is not None:
            self.dma_count[dma] += 16
            tok = (dma, self.dma_count[dma], eng, True)
            inc = (dma, 16)
            if is_output:
                self.out_dma.append(tok)
        else:
            self.count[eng] += 1
            tok = (eng, self.count[eng], eng, False)
            inc = (eng, 1)
        self.streams[eng].append((waits, fn, inc))
        for b in reads:
            old = b.r.get(tok[0])
            if old is None or old[1] < tok[1]:
                b.r[tok[0]] = tok
        for b in writes:
            b.w = tok
            b.r = {}
        return tok


def build_program(nt_prompt=4, with_sample=True, debug=False):
    nc = bass.Bass("TRN2", target_bir_lowering=False)
    NTOKP = nt_prompt * 512
    NS = SPC * DEC_S

    def din(name, shape):
        return nc.dram_tensor(name, list(shape), F32, kind="ExternalInput").ap()

    def dout(name, shape):
        return nc.dram_tensor(name, list(shape), F32, kind="ExternalOutput").ap()

    xp = din("xp", [NTOKP, D]); xs = din("xs", [NS, D])
    pp = din("pp", [NTOKP, PLE]); psm = din("ps", [NS, PLE])
    sca = din("sca", [SPC * 2, D]); spl = din("spl", [SPC * 15, D]); sff = din("sff", [SPC * 2, DFF])
    g_pre_mix = din("g_pre_mix", [D]); w_in = din("w_in", [D, 6 * D]); conv_a_w = din("conv_a_w", [3, D])
    w_a_out = din("w_a_out", [D, D]); pool_w = din("pool_w", [4, 256, 256]); pool_scale = din("pool_scale", [D])
    w_o = din("w_o", [D, D]); g_post_mix = din("g_post_mix", [D]); g_pre_ffn = din("g_pre_ffn", [D])
    w_up = din("w_up", [D, 2 * DFF]); ffn_conv_w = din("ffn_conv_w", [3, DFF]); w_down = din("w_down", [DFF, D])
    g_post_ffn = din("g_post_ffn", [D]); w_ple_proj = din("w_ple_proj", [PLE, D]); w_ple_gate = din("w_ple_gate", [D, D])

    yp = dout("yp", [NTOKP, D]); ys = dout("ys", [NS, D])
    ncp = dout("ncp", [2, D]); npp = dout("npp", [15, D]); nfp = dout("nfp", [2, DFF])
    ncs = dout("ncs", [SPC * 2, D]); nps = dout("nps", [SPC * 15, D]); nfs = dout("nfs", [SPC * 2, DFF])

    T = Tracker()
    from contextlib import ExitStack
    es = ExitStack()

    def sb(name, shape, dt=F32):
        return es.enter_context(nc.sbuf_tensor(name, list(shape), dt))

    xres = sb("xres", [128, 4, D]); xres_b = [Buf(f"xres{s}") for s in range(4)]
    xres_s = sb("xres_s", [128, 1, D]); xres_sb = [Buf("xres_s")]
    xb = [sb(f"xb{i}", [128, D], BF16) for i in range(2)]; xb_b = [Buf(f"xb{i}") for i in range(2)]
    actT = sb("actT", [128, 8, 512], BF16); actT_b = [Buf(f"actT_s{s}") for s in range(4)]
    big2 = sb("big", [128, 24 * 512], BF16); big_b = [[Buf(f"big{h}_{c}") for c in range(24)] for h in range(2)]
    big = big2[:, :].rearrange("p (c t) -> p c t", t=512)
    xstage = big2[:, :].bitcast(F32)

    def xst(s):
        return xstage[:, s * 1024:(s + 1) * 1024]

    def xst_b(s):
        return [big_b[h][i] for h in range(2) for i in range(4 * s, 4 * s + 4)]
    pT = sb("pT", [128, 2, 512], BF16); pT_b = [Buf(f"pT{h}") for h in range(2)]
    actT_s = sb("actT_s", [128, 8, 64], BF16); actT_sb = [Buf("actTs")]
    big_s = sb("big_s", [128, 24, 64], BF16); big_sb = [Buf(f"bigs_{c}") for c in range(24)]
    pT_s = sb("pT_s", [128, 2, 64], BF16); pT_sb = Buf("pTs")
    pf = sb("pf", [128, PLE]); pf_b = Buf("pf")
    pb = sb("pb", [128, PLE], BF16); pb_b = Buf("pb")
    temps = [sb(f"tmp{i}", [128, TEMPW]) for i in range(NTEMPS)]; temps_b = [Buf(f"tmp{i}") for i in range(NTEMPS)]
    junk = sb("junk", [128, D], BF16); junk_b = Buf("junk")
    tns = [sb(f"tn{i}", [128, D]) for i in range(2)]; tns_b = [Buf(f"tn{i}") for i in range(2)]
    tn_rr = [0]
    sg5 = [sb(f"sg5_{i}", [128, 512]) for i in range(2)]; sg5_b = [Buf(f"sg5_{i}") for i in range(2)]
    sg_rr = [0]
    stats = sb("stats", [128, 16]); stats_b = [Buf(f"st{i}") for i in range(16)]
    gbc = sb("gbc", [128, 4, D]); gbc_b = [Buf(f"gbc{i}") for i in range(4)]
    vrow = sb("vrow", [128, 128]); vrow_b = Buf("vrow")
    vecT = sb("vecT", [128, 128]); vecT_b = Buf("vecT")
    identf = sb("identf", [128, 128]); identb = sb("identb", [128, 128], BF16); ident_b = Buf("ident")
    iot = sb("iot", [128, 128], I32); iot_b = Buf("iot")
    rc16 = sb("rc16", [128, 4, 16]); rc16_b = Buf("rc16")
    io16 = sb("io16", [128, 16]); io16_b = Buf("io16")
    epsT = sb("epsT", [128, 1]); epsT_b = Buf("epsT")
    slots = [sb(f"slot{i}", [128, SLOT_ELEMS], BF16) for i in range(NSLOTS)]; slots_b = [Buf(f"slot{i}") for i in range(NSLOTS)]
    cA_p = sb("cA_p", [128, 8, 2]); cP_p = sb("cP_p", [128, 8, 15]); cF_p = sb("cF_p", [128, NFC, 2])
    cA_s = sb("cA_s", [128, 8, SPC * 2]); cP_s = sb("cP_s", [128, 8, SPC * 15]); cF_s = sb("cF_s", [128, NFC, SPC * 2])
    cA_pb = [Buf(f"cAp{c}") for c in range(8)]; cP_pb = [Buf(f"cPp{c}") for c in range(8)]; cF_pb = [Buf(f"cFp{c}") for c in range(NFC)]
    cA_sb = [Buf(f"cAs{c}") for c in range(8)]; cP_sb = [Buf(f"cPs{c}") for c in range(8)]; cF_sb = [Buf(f"cFs{c}") for c in range(NFC)]
    strow = tns[1]; strow_b = tns_b[1]

    banks = [es.enter_context(nc.psum_tensor(f"bank{i}", [128, 512], F32)) for i in range(8)]
    banks_b = [Buf(f"bank{i}", excl=True) for i in range(8)]
    bank_rr = [0]

    def take_banks(n):
        i = bank_rr[0]
        if i + n > 8:
            i = 0
        bank_rr[0] = (i + n) % 8
        return list(range(i, i + n))

    tmp_rr = [0]

    def take_tmp():
        i = tmp_rr[0]
        tmp_rr[0] = (i + 1) % NTEMPS
        return i

    st_rr = [0]

    def take_stat():
        i = st_rr[0]
        st_rr[0] = (i + 1) % 16
        return i

    xb_rr = [0]

    def ACT(fn, reads=(), writes=()):
        return T.emit("act", fn, reads, writes)

    def DVE(fn, reads=(), writes=()):
        return T.emit("dve", fn, reads, writes)

    def POOL(fn, reads=(), writes=()):
        return T.emit("pool", fn, reads, writes)

    stage = ["init"]
    pe_labels = []

    def PE(fn, reads=(), writes=()):
        pe_labels.append(stage[0])
        return T.emit("pe", fn, reads, writes)

    def DMA(eng, semname, fn, reads=(), writes=(), is_output=False):
        T.dma_sem(semname)
        return T.emit(eng, fn, reads, writes, dma=semname, is_output=is_output)

    CAW0 = 0
    PSC0 = 24
    FCW0 = 32

    DMA("sp", "c_gbc0", lambda e: e.dma_start(out=gbc[:, 0, :], in_=g_pre_mix.partition_broadcast(128)), writes=[gbc_b[0]])
    DMA("sp", "c_gbc1", lambda e: e.dma_start(out=gbc[:, 1, :], in_=g_post_mix.partition_broadcast(128)), writes=[gbc_b[1]])
    DMA("sp", "c_gbc2", lambda e: e.dma_start(out=gbc[:, 2, :], in_=g_pre_ffn.partition_broadcast(128)), writes=[gbc_b[2]])
    DMA("sp", "c_gbc3", lambda e: e.dma_start(out=gbc[:, 3, :], in_=g_post_ffn.partition_broadcast(128)), writes=[gbc_b[3]])
    POOL(lambda e: e.memset(vrow[:], 0.0), writes=[vrow_b])
    DMA("sp", "c_vrow", lambda e: e.dma_start(out=vrow[0:24, :], in_=conv_a_w.rearrange("k (c p) -> (k c) p", p=128)), writes=[vrow_b])
    DMA("sp", "c_vrow", lambda e: e.dma_start(out=vrow[24:32, :], in_=pool_scale.rearrange("(c p) -> c p", p=128)), writes=[vrow_b])
    DMA("sp", "c_vrow", lambda e: e.dma_start(out=vrow[32:98, :], in_=ffn_conv_w.rearrange("k (c p) -> (k c) p", p=128)), writes=[vrow_b])
    POOL(lambda e: e.iota(iot[:], pattern=[[1, 128]], base=0, channel_multiplier=-1), writes=[iot_b])
    DVE(lambda e: e.tensor_copy(out=identf[:], in_=iot[:]), reads=[iot_b], writes=[ident_b])
    DVE(lambda e: e.tensor_single_scalar(out=identf[:], in_=identf[:], scalar=0.0, op=ALU.is_equal), reads=[ident_b], writes=[ident_b])
    DVE(lambda e: e.tensor_copy(out=identb[:], in_=identf[:]), reads=[ident_b], writes=[ident_b])
    POOL(lambda e: e.iota(iot[:, 0:16], pattern=[[1, 16]], base=1, channel_multiplier=0), reads=[ident_b], writes=[iot_b])
    DVE(lambda e: e.tensor_copy(out=io16[:], in_=iot[:, 0:16]), reads=[iot_b], writes=[io16_b])
    for g, w in enumerate(POOLW):
        DVE(lambda e, g=g, w=w: e.tensor_scalar(out=rc16[:, g, :], in0=io16[:], scalar1=float(w), scalar2=None, op0=ALU.min),
            reads=[io16_b], writes=[rc16_b])
    DVE(lambda e: e.reciprocal(out=rc16[:], in_=rc16[:]), reads=[rc16_b], writes=[rc16_b])
    def emit_vecT():
        bk = take_banks(1)[0]
        PE(lambda e, bk=bk: e.transpose(out=banks[bk][:, 0:128], in_=vrow[:], identity=identf[:]), reads=[vrow_b, ident_b], writes=[banks_b[bk]])
        DVE(lambda e, bk=bk: e.tensor_copy(out=vecT[:], in_=banks[bk][:, 0:128]), reads=[banks_b[bk]], writes=[vecT_b])
    POOL(lambda e: e.memset(epsT[:], EPS), writes=[epsT_b])
    POOL(lambda e: e.memset(cA_p[:], 0.0), writes=cA_pb)
    POOL(lambda e: e.memset(cP_p[:], 0.0), writes=cP_pb)
    POOL(lambda e: e.memset(cF_p[:], 0.0), writes=cF_pb)

    def sview(slot, nk, ncols):
        return slot[:, 0:nk * ncols].rearrange("p (k n) -> p k n", n=ncols)

    def wsrc(w, k0, nk, c0, ncols):
        return w[k0 * 128:(k0 + nk) * 128, c0:c0 + ncols].rearrange("(k p) n -> p k n", p=128)

    tile_blocks = []
    tile_blocks.append(("H", [(lambda sl: sview(sl, 8, D), wsrc(w_in, 0, 8, 2 * D, D))], 8 * D))
    for jb in range(2):
        tile_blocks.append((f"BC{jb}", [(lambda sl, s=s: sview(sl, 8, 1024)[:, :, s * 512:(s + 1) * 512], wsrc(w_in, 0, 8, s * D + 512 * jb, 512)) for s in range(2)], 8 * 1024))
        if jb == 0:
            tile_blocks.append(("U", [(lambda sl: sview(sl, 8, D), wsrc(w_in, 0, 8, 3 * D, D))], 8 * D))

    def gblock(jg):
        return (f"G{jg}", [(lambda sl, s=s: sview(sl, 8, 1024)[:, :, s * 512:(s + 1) * 512], wsrc(w_in, 0, 8, (4 + s) * D + 512 * jg, 512)) for s in range(2)], 8 * 1024)
    tile_blocks.append(gblock(0))
    tile_blocks.append(("AO", [(lambda sl: sview(sl, 8, D), wsrc(w_a_out, 0, 8, 0, D))], 8 * D))
    tile_blocks.append(("PW", [(lambda sl: sview(sl, 8, 256), pool_w.rearrange("g (k p) n -> p (g k) n", p=128))], 8 * 256))
    tile_blocks.append(gblock(1))
    tile_blocks.append(("O", [(lambda sl: sview(sl, 8, D), wsrc(w_o, 0, 8, 0, D))], 8 * D))
    for j in range(6):
        ncol = 512 if j < 5 else 256
        tile_blocks.append((f"UP{j}", [(lambda sl, s=s, ncol=ncol: sview(sl, 8, 1024)[:, :, s * 512:s * 512 + ncol], wsrc(w_up, 0, 8, s * DFF + 512 * j, ncol)) for s in range(2)], 8 * 1024))
    DKB = [(0, 8), (8, 8), (16, 6)]
    for kb, (k0, nk) in enumerate(DKB):
        tile_blocks.append((f"D{kb}", [(lambda sl, nk=nk: sview(sl, nk, D), wsrc(w_down, k0, nk, 0, D))], nk * D))
    tile_blocks.append(("E1", [(lambda sl: sview(sl, 8, D), wsrc(w_ple_gate, 0, 8, 0, D))], 8 * D))
    tile_blocks.append(("E2", [(lambda sl: sview(sl, 2, D), wsrc(w_ple_proj, 0, 2, 0, D))], 2 * D))
    NBLK = len(tile_blocks)
    n_tiles_total = nt_prompt
    NG = NBLK * n_tiles_total
    wscr = nc.dram_tensor("wscratch", [NBLK, 128, SLOT_ELEMS], BF16, kind="Internal").ap()
    wscr_b = [Buf(f"wscr{i}") for i in range(NBLK)]
    ws = {"next_load": 0, "next_acq": 0, "free": list(range(NSLOTS)), "slot_of": {}}

    def emit_load(j, si):
        bi = j % NBLK
        name, dmas, nel = tile_blocks[bi]
        if j < NBLK:
            for dst_fn, src in dmas:
                DMA("pool", f"w_slot{si}", lambda e, dst_fn=dst_fn, src=src, si=si: e.dma_start(out=dst_fn(slots[si]), in_=src), writes=[slots_b[si]])
            if n_tiles_total > 1:
                DMA("sp", f"w_wb{bi}", lambda e, si=si, bi=bi, nel=nel: e.dma_start(out=wscr[bi, :, 0:nel], in_=slots[si][:, 0:nel]), reads=[slots_b[si]], writes=[wscr_b[bi]])
        else:
            DMA("sp", f"w_slot{si}", lambda e, si=si, bi=bi, nel=nel: e.dma_start(out=slots[si][:, 0:nel], in_=wscr[bi, :, 0:nel]), reads=[wscr_b[bi]], writes=[slots_b[si]])

    def pump_loads():
        while ws["next_load"] < NG and ws["free"]:
            si = ws["free"].pop(0)
            ws["slot_of"][ws["next_load"]] = si
            emit_load(ws["next_load"], si)
            ws["next_load"] += 1

    def acquire(name):
        j = ws["next_acq"]
        assert tile_blocks[j % NBLK][0] == name, (tile_blocks[j % NBLK][0], name)
        ws["next_acq"] += 1
        pump_loads()
        assert ws["next_load"] > j, "weight ring deadlock: block %d (%s) not loadable" % (j, name)
        return j, ws["slot_of"][j]

    def release(j):
        ws["free"].append(ws["slot_of"][j])
        pump_loads()

    class Ctx:
        pass

    def make_prompt_ctx(tidx, h=None):
        c = Ctx()
        hs = (0, 1) if h is None else (h,)
        c.kind = "prompt"; c.key = "p" + ("f" if h is None else "ab"[h])
        c.NSUB = 2 * len(hs); c.TOK = 256 * len(hs); c.L = c.TOK; c.nseq = 1; c.R = 128
        col0 = 256 * hs[0]
        sub0 = 2 * hs[0]
        r0 = tidx * 512 + col0
        c.x_src = xp[r0:r0 + c.TOK, :]; c.p_src = pp[r0:r0 + c.TOK, :]; c.y_dst = yp[r0:r0 + c.TOK, :]
        c.xres = lambda s, sub0=sub0: xres[0:128, sub0 + s, :]
        c.xres_b = xres_b[sub0:sub0 + c.NSUB]
        c.xkey = lambda s, sub0=sub0: f"p{sub0 + s}"
        cs = slice(col0, col0 + c.TOK)
        c.actT = actT[:, :, cs]; c.actT_b = actT_b[sub0:sub0 + c.NSUB]
        c.actT_sub = lambda s, sub0=sub0: [actT_b[sub0 + s]]
        c.za = big[:, 0:8, cs]; c.dd = big[:, 8:16, cs]; c.mg = big[:, 16:24, cs]; c.ff = big[:, 0:NFC, cs]
        c.bufs = lambda lo, hi: [b_ for h_ in hs for b_ in big_b[h_][lo:hi]]
        c.buf1 = lambda i: [big_b[h_][i] for h_ in hs]
        c.pT = pT[:, :, cs]; c.pT_b = [pT_b[h_] for h_ in hs]
        c.cA, c.cP, c.cF = cA_p, cP_p, cF_p
        c.cA_b, c.cP_b, c.cF_b = cA_pb, cP_pb, cF_pb
        c.first = (tidx == 0 and hs[0] == 0)
        c.g0 = (tidx == 0)
        return c

    def make_sample_ctx():
        c = Ctx()
        c.kind = "sample"; c.key = "s"
        c.L = DEC_S; c.nseq = SPC; c.R = NS; c.NSUB = 1; c.TOK = NS
        c.x_src = xs; c.p_src = psm; c.y_dst = ys
        c.xres = lambda s: xres_s[0:NS, 0, :]
        c.xres_b = xres_sb
        c.xkey = lambda s: "s"
        c.actT = actT_s[:, :, :]; c.actT_b = actT_sb
        c.actT_sub = lambda s: actT_sb
        c.za = big_s[:, 0:8, :]; c.dd = big_s[:, 8:16, :]; c.mg = big_s[:, 16:24, :]; c.ff = big_s[:, 0:NFC, :]
        c.bufs = lambda lo, hi: big_sb[lo:hi]
        c.buf1 = lambda i: [big_sb[i]]
        c.pT = pT_s[:, :, :]; c.pT_b = [pT_sb]
        c.cA, c.cP, c.cF = cA_s, cP_s, cF_s
        c.cA_b, c.cP_b, c.cF_b = cA_sb, cP_sb, cF_sb
        c.first = False
        c.g0 = True
        return c

    def PO(c):
        return DVE if c.g0 else POOL

    def DQ(c):
        return "pool"

    def ext_view(c, ti, pre):
        n = c.nseq * (pre + c.L)
        return temps[ti][:, 0:n].rearrange("p (s j) -> p s j", j=pre + c.L)

    def tok_view(c, ap2d):
        return ap2d.rearrange("p (s j) -> p s j", j=c.L)

    def carry_view(c, carr, ch, pre):
        return carr[:, ch, :].rearrange("p (s j) -> p s j", j=pre)

    def row_rstd(c, src_ap, src_bufs):
        si = take_stat()
        ACT(lambda e, si=si: e.memzero(stats[:, si:si + 1]), writes=[stats_b[si]])
        ACT(lambda e, si=si: e.activation(out=junk[0:c.R, :], in_=src_ap, func=AF.Square, scale=1.0 / 32.0, accum_out=stats[0:c.R, si:si + 1]),
            reads=list(src_bufs), writes=[junk_b, stats_b[si]])
        ACT(lambda e, si=si: e.activation(out=stats[0:c.R, si:si + 1], in_=stats[0:c.R, si:si + 1], func=AF.Sqrt, bias=epsT[0:c.R, 0:1], scale=1.0),
            reads=[stats_b[si], epsT_b], writes=[stats_b[si]])
        DVE(lambda e, si=si: e.reciprocal(out=stats[0:c.R, si:si + 1], in_=stats[0:c.R, si:si + 1]), reads=[stats_b[si]], writes=[stats_b[si]])
        return si

    def norm_to_bf16(c, s, gidx, src=None, src_b=None):
        src = c.xres(s) if src is None else src
        src_b = [c.xres_b[s]] if src_b is None else src_b
        si = row_rstd(c, src, src_b)
        xi = xb_rr[0]
        xb_rr[0] = (xi + 1) % 2
        DVE(lambda e, si=si, xi=xi: e.scalar_tensor_tensor(out=xb[xi][0:c.R, :], in0=src, scalar=stats[0:c.R, si:si + 1],
                                                             in1=gbc[0:c.R, gidx, :], op0=ALU.mult, op1=ALU.mult),
            reads=list(src_b) + [stats_b[si], gbc_b[gidx]], writes=[xb_b[xi]])
        return xi

    def transposes_sub(c, s, src_ap, src_bufs, nchunks, dst, dst_bufs, evac_dve=False, defer_evac=False):
        bk = take_banks(1)[0]
        pv = banks[bk][:].bitcast(BF16)

        def fn(e):
            ins = None
            for ch in range(nchunks):
                ins = e.transpose(out=pv[:, ch * 128:ch * 128 + c.R], in_=src_ap[:, ch * 128:(ch + 1) * 128], identity=identb[0:c.R, 0:c.R])
            return ins
        PE(fn, reads=list(src_bufs) + [ident_b], writes=[banks_b[bk]])
        pv3 = pv[:, 0:nchunks * 128].rearrange("p (k t) -> p k t", t=128)
        def evac():
            if evac_dve:
                DVE(lambda e: e.tensor_copy(out=dst[:, 0:nchunks, s * 128:s * 128 + c.R], in_=pv3[:, :, 0:c.R]), reads=[banks_b[bk]], writes=list(dst_bufs))
            else:
                ACT(lambda e: e.copy(out=dst[:, 0:nchunks, s * 128:s * 128 + c.R], in_=pv3[:, :, 0:c.R]), reads=[banks_b[bk]], writes=list(dst_bufs))
        if defer_evac:
            return evac
        evac()
        return None

    stg_rr = [0]

    pre_state = {}

    def state_dma(src, name, r0, rn, c0, wg_):
        bi_ = stg_rr[0]; stg_rr[0] = 1 - bi_
        stg = tns[bi_]; stg_b = tns_b[bi_]
        DMA("pool", f"st_in{bi_}", lambda e, r0=r0, rn=rn, c0=c0, wg_=wg_, stg=stg: e.dma_start(out=stg[0:rn, 0:wg_], in_=src[r0:r0 + rn, c0:c0 + wg_]),
            writes=[stg_b])
        return stg, stg_b

    def load_state(src, nrows, width, carr, carr_b, name=None):
        rt = [(0, min(128, nrows))] + ([(128, nrows - 128)] if nrows > 128 else [])
        for c0 in range(0, width, D):
            wg_ = min(D, width - c0)
            for (r0, rn) in rt:
                if (name, r0, c0) in pre_state:
                    stg, stg_b = pre_state.pop((name, r0, c0))
                else:
                    stg, stg_b = state_dma(src, name, r0, rn, c0, wg_)
                for ci in range(wg_ // 128):
                    ch = c0 // 128 + ci
                    bk = take_banks(1)[0]
                    PE(lambda e, ci=ci, bk=bk, rn=rn, stg=stg: e.transpose(out=banks[bk][:, 0:rn], in_=stg[0:rn, ci * 128:(ci + 1) * 128], identity=identf[0:rn, 0:rn]),
                       reads=[stg_b, ident_b], writes=[banks_b[bk]])
                    DVE(lambda e, ch=ch, bk=bk, r0=r0, rn=rn: e.tensor_copy(out=carr[:, ch, r0:r0 + rn], in_=banks[bk][:, 0:rn]),
                        reads=[banks_b[bk]], writes=[carr_b[ch]])

    def s_pre(c, what):
        if c.kind != "sample":
            return
        if what == "1A":
            load_state(sca, SPC * 2, D, cA_s, cA_sb, name="sca")
        elif what == "1P":
            load_state(spl, SPC * 15, D, cP_s, cP_sb, name="spl")
        elif what == "3":
            load_state(sff, SPC * 2, DFF, cF_s, cF_sb, name="sff")

    def s_norm_T_sub(c, s, gidx, defer_evac=False):
        xi = norm_to_bf16(c, s, gidx)
        ev = transposes_sub(c, s, xb[xi][0:c.R, :], [xb_b[xi]], 8, c.actT, c.actT_sub(s), defer_evac=defer_evac)
        return [ev] if ev is not None else []

    def s_load_sub(c, s):
        R = c.R
        DMA(DQ(c), f"x_in_{c.xkey(s)}", lambda e, s=s: e.dma_start(out=c.xres(s), in_=c.x_src[s * R:(s + 1) * R, :]), writes=[c.xres_b[s]])

    def s_1A(c, jb, ccs, wbc, wh, wbufs):
        L, TOK = c.L, c.TOK
        for cc in ccs:
            ch = 4 * jb + cc
            bks = take_banks(3)

            def fn(e, cc=cc, ch=ch, bks=bks):
                ins = None
                for s in range(3):
                    for k in range(8):
                        lw = wbc[:, k, s * 512 + cc * 128:s * 512 + (cc + 1) * 128] if s < 2 else wh[:, k, ch * 128:(ch + 1) * 128]
                        ins = e.matmul(banks[bks[s]][:, 0:TOK], lhsT=lw, rhs=c.actT[:, k, :], start=(k == 0), stop=(k == 7))
                return ins
            PE(fn, reads=list(wbufs) + list(c.actT_b), writes=[banks_b[b] for b in bks])
            bB, bC, bH = bks
            t_c = take_tmp(); t_e = take_tmp(); t_v = take_tmp()
            ACT(lambda e, t_c=t_c, bC=bC: e.copy(out=temps[t_c][:, 0:TOK], in_=banks[bC][:, 0:TOK]), reads=[banks_b[bC]], writes=[temps_b[t_c]])
            ev = ext_view(c, t_e, 2)
            PO(c)(lambda e, ev=ev, ch=ch: e.tensor_copy(out=ev[:, :, 0:2], in_=carry_view(c, c.cA, ch, 2)), reads=[c.cA_b[ch]], writes=[temps_b[t_e]])
            DVE(lambda e, ev=ev, bH=bH, t_c=t_c: e.tensor_tensor(out=ev[:, :, 2:2 + L], in0=tok_view(c, banks[bH][:, 0:TOK]),
                                                                 in1=tok_view(c, temps[t_c][:, 0:TOK]), op=ALU.mult),
                reads=[banks_b[bH], temps_b[t_c]], writes=[temps_b[t_e]])
            PO(c)(lambda e, ev=ev, ch=ch: e.tensor_copy(out=carry_view(c, c.cA, ch, 2), in_=ev[:, :, L:L + 2]), reads=[temps_b[t_e]], writes=[c.cA_b[ch]])
            vv = tok_view(c, temps[t_v][:, 0:TOK])
            ACT(lambda e, ev=ev, vv=vv, ch=ch: e.activation(out=vv, in_=ev[:, :, 0:L], func=AF.Copy, scale=vecT[:, CAW0 + ch:CAW0 + ch + 1]),
                reads=[temps_b[t_e], vecT_b], writes=[temps_b[t_v]])
            for k in (1, 2):
                DVE(lambda e, ev=ev, vv=vv, ch=ch, k=k: e.scalar_tensor_tensor(out=vv, in0=ev[:, :, k:k + L], scalar=vecT[:, CAW0 + k * 8 + ch:CAW0 + k * 8 + ch + 1],
                                                                                 in1=vv, op0=ALU.mult, op1=ALU.add),
                    reads=[temps_b[t_e], temps_b[t_v], vecT_b], writes=[temps_b[t_v]])
            DVE(lambda e, ch=ch, bB=bB, t_v=t_v: e.tensor_tensor(out=c.za[:, ch, :], in0=banks[bB][:, 0:TOK], in1=temps[t_v][:, 0:TOK], op=ALU.mult),
                reads=[banks_b[bB], temps_b[t_v]], writes=c.buf1(ch))

    def s_1P(c, wv, wbuf, chs=range(8)):
        L, TOK = c.L, c.TOK
        for ch in chs:
            g = ch // 2
            w = POOLW[g]
            bk = take_banks(1)[0]

            def fn(e, ch=ch, bk=bk):
                ins = None
                for k in range(8):
                    ins = e.matmul(banks[bk][:, 0:TOK], lhsT=wv[:, k, ch * 128:(ch + 1) * 128], rhs=c.actT[:, k, :], start=(k == 0), stop=(k == 7))
                return ins
            PE(fn, reads=[wbuf] + list(c.actT_b), writes=[banks_b[bk]])
            t_u = take_tmp()
            uv = ext_view(c, t_u, 15)
            PO(c)(lambda e, uv=uv, ch=ch: e.tensor_copy(out=uv[:, :, 0:15], in_=carry_view(c, c.cP, ch, 15)), reads=[c.cP_b[ch]], writes=[temps_b[t_u]])
            ACT(lambda e, uv=uv, bk=bk: e.copy(out=uv[:, :, 15:15 + L], in_=tok_view(c, banks[bk][:, 0:TOK])), reads=[banks_b[bk]], writes=[temps_b[t_u]])
            PO(c)(lambda e, uv=uv, ch=ch: e.tensor_copy(out=carry_view(c, c.cP, ch, 15), in_=uv[:, :, L:L + 15]), reads=[temps_b[t_u]], writes=[c.cP_b[ch]])
            prev = uv; prev_t = t_u
            sh = 1
            while sh < w:
                t_s = take_tmp()
                sv = ext_view(c, t_s, 15)
                lo = 2 * sh - 1
                (PO(c) if sh == 1 else DVE)(lambda e, sv=sv, prev=prev, lo=lo, sh=sh: e.tensor_tensor(out=sv[:, :, lo:15 + L], in0=prev[:, :, lo:15 + L],
                                                                                                       in1=prev[:, :, lo - sh:15 + L - sh], op=ALU.add),
                                           reads=[temps_b[prev_t]], writes=[temps_b[t_s]])
                prev = sv; prev_t = t_s
                sh *= 2
            DVE(lambda e, prev=prev, uv=uv, ch=ch, w=w: e.scalar_tensor_tensor(out=tok_view(c, c.dd[:, ch, :]), in0=prev[:, :, 15:15 + L], scalar=1.0 / w,
                                                                                in1=uv[:, :, 15:15 + L], op0=ALU.mult, op1=ALU.subtract),
                reads=[temps_b[prev_t], temps_b[t_u]], writes=c.buf1(8 + ch))
            if c.first:
                t_f = take_tmp()
                DVE(lambda e, prev=prev, g=g, t_f=t_f: e.tensor_tensor(out=temps[t_f][:, 0:16], in0=prev[:, 0, 15:31], in1=rc16[:, g, :], op=ALU.mult),
                    reads=[temps_b[prev_t], rc16_b], writes=[temps_b[t_f]])
                DVE(lambda e, uv=uv, ch=ch, t_f=t_f: e.tensor_tensor(out=c.dd[:, ch, 0:16], in0=temps[t_f][:, 0:16], in1=uv[:, 0, 15:31], op=ALU.subtract),
                    reads=[temps_b[t_f], temps_b[t_u]], writes=c.buf1(8 + ch))

    def s_1G(c, jg, wg, wa, wp, wbufs):
        TOK = c.TOK
        for cc in range(4):
            ch = 4 * jg + cc
            g = ch // 2
            bks = take_banks(4)

            def fn(e, cc=cc, ch=ch, g=g, bks=bks):
                ins = None
                for s in range(2):
                    for k in range(8):
                        ins = e.matmul(banks[bks[s]][:, 0:TOK], lhsT=wg[:, k, s * 512 + cc * 128:s * 512 + (cc + 1) * 128], rhs=c.actT[:, k, :],
                                       start=(k == 0), stop=(k == 7))
                for k in range(8):
                    ins = e.matmul(banks[bks[2]][:, 0:TOK], lhsT=wa[:, k, ch * 128:(ch + 1) * 128], rhs=c.za[:, k, :], start=(k == 0), stop=(k == 7))
                for k in range(2):
                    ins = e.matmul(banks[bks[3]][:, 0:TOK], lhsT=wp[:, g * 2 + k, (ch % 2) * 128:(ch % 2 + 1) * 128], rhs=c.dd[:, g * 2 + k, :],
                                   start=(k == 0), stop=(k == 1))
                return ins
            PE(fn, reads=list(wbufs) + list(c.actT_b) + c.bufs(0, 16), writes=[banks_b[b] for b in bks])
            t_a = take_tmp(); t_p = take_tmp()
            ACT(lambda e, t_a=t_a, b=bks[0]: e.activation(out=temps[t_a][:, 0:TOK], in_=banks[b][:, 0:TOK], func=AF.Sigmoid), reads=[banks_b[bks[0]]], writes=[temps_b[t_a]])
            ACT(lambda e, t_p=t_p, b=bks[1]: e.activation(out=temps[t_p][:, 0:TOK], in_=banks[b][:, 0:TOK], func=AF.Sigmoid), reads=[banks_b[bks[1]]], writes=[temps_b[t_p]])
            DVE(lambda e, t_a=t_a, b=bks[2]: e.tensor_tensor(out=temps[t_a][:, 0:TOK], in0=banks[b][:, 0:TOK], in1=temps[t_a][:, 0:TOK], op=ALU.mult),
                reads=[banks_b[bks[2]], temps_b[t_a]], writes=[temps_b[t_a]])
            DVE(lambda e, t_p=t_p, b=bks[3], ch=ch: e.scalar_tensor_tensor(out=temps[t_p][:, 0:TOK], in0=banks[b][:, 0:TOK], scalar=vecT[:, PSC0 + ch:PSC0 + ch + 1],
                                                                             in1=temps[t_p][:, 0:TOK], op0=ALU.mult, op1=ALU.mult),
                reads=[banks_b[bks[3]], temps_b[t_p], vecT_b], writes=[temps_b[t_p]])
            DVE(lambda e, t_a=t_a, t_p=t_p, ch=ch: e.tensor_tensor(out=c.mg[:, ch, :], in0=temps[t_a][:, 0:TOK], in1=temps[t_p][:, 0:TOK], op=ALU.add),
                reads=[temps_b[t_a], temps_b[t_p]], writes=c.buf1(16 + ch))

    def s_tm(c, lhs, lhs_b, nk, wv_fn, w_bufs, gidx, post=None, lag=1):
        R = c.R
        for s in range(c.NSUB + lag):
            if s >= c.NSUB:
                if post is not None and s - lag >= 0:
                    for ev in (post(s - lag) or []):
                        ev()
                continue
            bks = take_banks(2)

            def fn(e, s=s, bks=bks):
                ins = None
                for hf in range(2):
                    for k in range(nk):
                        ins = e.matmul(banks[bks[hf]][0:R, :], lhsT=lhs[:, k, s * 128:s * 128 + R], rhs=wv_fn(k, hf), start=(k == 0), stop=(k == nk - 1))
                return ins
            PE(fn, reads=list(w_bufs) + list(lhs_b), writes=[banks_b[b] for b in bks])
            late = []
            if s >= lag and post is not None:
                late = post(s - lag) or []
            sa = take_stat(); sbb = take_stat()
            ti_ = tn_rr[0]; tn_rr[0] = 1 - ti_
            tn = tns[ti_]; tn_b = tns_b[ti_]
            ACT(lambda e, sa=sa: e.memzero(stats[:, sa:sa + 1]), writes=[stats_b[sa]])
            ACT(lambda e, sbb=sbb: e.memzero(stats[:, sbb:sbb + 1]), writes=[stats_b[sbb]])
            ACT(lambda e, sa=sa, b=bks[0]: e.activation(out=junk[0:R, 0:512], in_=banks[b][0:R, :], func=AF.Square, scale=1.0 / 32.0, accum_out=stats[0:R, sa:sa + 1]),
                reads=[banks_b[bks[0]]], writes=[junk_b, stats_b[sa]])
            DVE(lambda e, b=bks[1], tn=tn: e.tensor_tensor(out=tn[0:R, 512:1024], in0=banks[b][0:R, :], in1=gbc[0:R, gidx, 512:1024], op=ALU.mult),
                reads=[banks_b[bks[1]], gbc_b[gidx]], writes=[tn_b])
            ACT(lambda e, sbb=sbb, b=bks[1]: e.activation(out=junk[0:R, 512:1024], in_=banks[b][0:R, :], func=AF.Square, scale=1.0 / 32.0, accum_out=stats[0:R, sbb:sbb + 1]),
                reads=[banks_b[bks[1]]], writes=[junk_b, stats_b[sbb]])
            DVE(lambda e, b=bks[0], tn=tn: e.tensor_tensor(out=tn[0:R, 0:512], in0=banks[b][0:R, :], in1=gbc[0:R, gidx, 0:512], op=ALU.mult),
                reads=[banks_b[bks[0]], gbc_b[gidx]], writes=[tn_b])
            DVE(lambda e, sa=sa, sbb=sbb: e.tensor_tensor(out=stats[0:R, sa:sa + 1], in0=stats[0:R, sa:sa + 1], in1=stats[0:R, sbb:sbb + 1], op=ALU.add),
                reads=[stats_b[sa], stats_b[sbb]], writes=[stats_b[sa]])
            ACT(lambda e, sa=sa: e.activation(out=stats[0:R, sa:sa + 1], in_=stats[0:R, sa:sa + 1], func=AF.Sqrt, bias=epsT[0:R, 0:1], scale=1.0),
                reads=[stats_b[sa], epsT_b], writes=[stats_b[sa]])
            DVE(lambda e, sa=sa: e.reciprocal(out=stats[0:R, sa:sa + 1], in_=stats[0:R, sa:sa + 1]), reads=[stats_b[sa]], writes=[stats_b[sa]])
            DVE(lambda e, s=s, sa=sa, tn=tn: e.scalar_tensor_tensor(out=c.xres(s), in0=tn[0:R, :], scalar=stats[0:R, sa:sa + 1], in1=c.xres(s),
                                                                     op0=ALU.mult, op1=ALU.add),
                reads=[c.xres_b[s], tn_b, stats_b[sa]], writes=[c.xres_b[s]])
            for ev in late:
                ev()

    def s_3_sample(c, j, nch, wu, wbuf):
        L, TOK = c.L, c.TOK
        ch0 = 4 * j
        W = nch * TOK
        bA, bG = take_banks(2)

        def fn(e):
            ins = None
            for s, bk in ((0, bA), (1, bG)):
                for cc in range(nch):
                    for k in range(8):
                        ins = e.matmul(banks[bk][:, cc * TOK:(cc + 1) * TOK], lhsT=wu[:, k, s * 512 + cc * 128:s * 512 + (cc + 1) * 128], rhs=c.actT[:, k, :],
                                       start=(k == 0), stop=(k == 7))
            return ins
        PE(fn, reads=[wbuf] + list(c.actT_b), writes=[banks_b[bA], banks_b[bG]])
        t_e = take_tmp(); t_v = take_tmp()
        n6 = c.nseq * (L + 2)
        ev4 = temps[t_e][:, 0:nch * n6].rearrange("p (c s j) -> p c s j", c=nch, j=L + 2)
        cfv = c.cF[:, ch0:ch0 + nch, :].rearrange("p c (s j) -> p c s j", j=2)
        cbufs = list(c.cF_b[ch0:ch0 + nch])
        PO(c)(lambda e: e.tensor_copy(out=ev4[:, :, :, 0:2], in_=cfv), reads=cbufs, writes=[temps_b[t_e]])
        ACT(lambda e: e.copy(out=ev4[:, :, :, 2:2 + L], in_=banks[bA][:, 0:W].rearrange("p (c s j) -> p c s j", c=nch, j=L)), reads=[banks_b[bA]], writes=[temps_b[t_e]])
        PO(c)(lambda e: e.tensor_copy(out=cfv, in_=ev4[:, :, :, L:L + 2]), reads=[temps_b[t_e]], writes=cbufs)
        for cc in range(nch):
            ch = ch0 + cc
            ev = ev4[:, cc, :, :]
            vv = temps[t_v][:, cc * TOK:(cc + 1) * TOK].rearrange("p (s j) -> p s j", j=L)
            ACT(lambda e, ev=ev, vv=vv, ch=ch: e.activation(out=vv, in_=ev[:, :, 0:L], func=AF.Copy, scale=vecT[:, FCW0 + ch:FCW0 + ch + 1]),
                reads=[temps_b[t_e], vecT_b], writes=[temps_b[t_v]])
            for k in (1, 2):
                DVE(lambda e, ev=ev, vv=vv, ch=ch, k=k: e.scalar_tensor_tensor(out=vv, in0=ev[:, :, k:k + L], scalar=vecT[:, FCW0 + k * NFC + ch:FCW0 + k * NFC + ch + 1],
                                                                                 in1=vv, op0=ALU.mult, op1=ALU.add),
                    reads=[temps_b[t_e], temps_b[t_v], vecT_b], writes=[temps_b[t_v]])
        ACT(lambda e: e.activation(out=temps[t_v][:, 0:W], in_=temps[t_v][:, 0:W], func=AF.Gelu_apprx_tanh), reads=[temps_b[t_v]], writes=[temps_b[t_v]])
        DVE(lambda e: e.tensor_tensor(out=c.ff[:, ch0:ch0 + nch, :], in0=banks[bG][:, 0:W].rearrange("p (c t) -> p c t", c=nch),
                                      in1=temps[t_v][:, 0:W].rearrange("p (c t) -> p c t", c=nch), op=ALU.mult),
            reads=[banks_b[bG], temps_b[t_v]], writes=c.bufs(ch0, ch0 + nch))

    def s_3(c, j, nch, wu, wbuf):
        if c.kind == "sample":
            return s_3_sample(c, j, nch, wu, wbuf)
        L, TOK = c.L, c.TOK
        pend = c.__dict__.setdefault("s3_pend", [])
        for cc in range(nch):
            ch = 4 * j + cc
            bks = take_banks(2)

            def fn(e, cc=cc, bks=bks):
                ins = None
                for s in range(2):
                    for k in range(8):
                        ins = e.matmul(banks[bks[s]][:, 0:TOK], lhsT=wu[:, k, s * 512 + cc * 128:s * 512 + (cc + 1) * 128], rhs=c.actT[:, k, :],
                                       start=(k == 0), stop=(k == 7))
                return ins
            PE(fn, reads=[wbuf] + list(c.actT_b), writes=[banks_b[b] for b in bks])
            bA, bG = bks
            t_e = take_tmp(); t_v = take_tmp()
            ev = ext_view(c, t_e, 2)
            PO(c)(lambda e, ev=ev, ch=ch: e.tensor_copy(out=ev[:, :, 0:2], in_=carry_view(c, c.cF, ch, 2)), reads=[c.cF_b[ch]], writes=[temps_b[t_e]])
            ACT(lambda e, ev=ev, bA=bA: e.copy(out=ev[:, :, 2:2 + L], in_=tok_view(c, banks[bA][:, 0:TOK])), reads=[banks_b[bA]], writes=[temps_b[t_e]])
            PO(c)(lambda e, ev=ev, ch=ch: e.tensor_copy(out=carry_view(c, c.cF, ch, 2), in_=ev[:, :, L:L + 2]), reads=[temps_b[t_e]], writes=[c.cF_b[ch]])
            vv = tok_view(c, temps[t_v][:, 0:TOK])
            ACT(lambda e, ev=ev, vv=vv, ch=ch: e.activation(out=vv, in_=ev[:, :, 0:L], func=AF.Copy, scale=vecT[:, FCW0 + ch:FCW0 + ch + 1]),
                reads=[temps_b[t_e], vecT_b], writes=[temps_b[t_v]])
            for k in (1, 2):
                DVE(lambda e, ev=ev, vv=vv, ch=ch, k=k: e.scalar_tensor_tensor(out=vv, in0=ev[:, :, k:k + L], scalar=vecT[:, FCW0 + k * NFC + ch:FCW0 + k * NFC + ch + 1],
                                                                                 in1=vv, op0=ALU.mult, op1=ALU.add),
                    reads=[temps_b[t_e], temps_b[t_v], vecT_b], writes=[temps_b[t_v]])
            s_3_flush(c)
            pend.append((ch, bG, t_v))
        s_3_flush(c)

    def s_3_flush(c):
        TOK = c.TOK
        pend = c.__dict__.setdefault("s3_pend", [])
        while pend:
            ch, bG, t_v = pend.pop(0)
            ACT(lambda e, t_v=t_v: e.activation(out=temps[t_v][:, 0:TOK], in_=temps[t_v][:, 0:TOK], func=AF.Gelu_apprx_tanh), reads=[temps_b[t_v]], writes=[temps_b[t_v]])
            DVE(lambda e, ch=ch, bG=bG, t_v=t_v: e.tensor_tensor(out=c.ff[:, ch, :], in0=banks[bG][:, 0:TOK], in1=temps[t_v][:, 0:TOK], op=ALU.mult),
                reads=[banks_b[bG], temps_b[t_v]], writes=c.buf1(ch))

    def s_5pre_sub(c, s):
        R = c.R
        if True:
            xi = xb_rr[0]
            xb_rr[0] = (xi + 1) % 2
            ACT(lambda e, s=s, xi=xi: e.copy(out=xb[xi][0:R, :], in_=c.xres(s)), reads=[c.xres_b[s]], writes=[xb_b[xi]])
            ev1 = transposes_sub(c, s, xb[xi][0:R, :], [xb_b[xi]], 8, c.actT, c.actT_sub(s), defer_evac=True)
            DMA(DQ(c), "p_in", lambda e, s=s: e.dma_start(out=pf[0:R, :], in_=c.p_src[s * R:(s + 1) * R, :]), writes=[pf_b])
            DVE(lambda e: e.tensor_copy(out=pb[0:R, :], in_=pf[0:R, :]), reads=[pf_b], writes=[pb_b])
            ev2 = transposes_sub(c, s, pb[0:R, :], [pb_b], 2, c.pT, c.pT_b, defer_evac=True)
            return [ev1, ev2]

    def s_5(c, wgt, wpr, wbuf, post_a=None, post_b=None):
        R = c.R
        for s in range(c.NSUB):
            grp = []
            for hf in range(2):
                bks = take_banks(2)

                def fn(e, s=s, hf=hf, bks=bks):
                    ins = None
                    for k in range(8):
                        ins = e.matmul(banks[bks[0]][0:R, :], lhsT=c.actT[:, k, s * 128:s * 128 + R], rhs=wgt[:, k, hf * 512:(hf + 1) * 512], start=(k == 0), stop=(k == 7))
                    for k in range(2):
                        ins = e.matmul(banks[bks[1]][0:R, :], lhsT=c.pT[:, k, s * 128:s * 128 + R], rhs=wpr[:, k, hf * 512:(hf + 1) * 512], start=(k == 0), stop=(k == 1))
                    return ins
                PE(fn, reads=list(wbuf) + list(c.pT_b) + c.actT_sub(s), writes=[banks_b[b] for b in bks])
                grp.append(bks)
            if post_a is not None:
                if s >= 1:
                    post_a(s - 1)
                if s == c.NSUB - 1:
                    post_a(s)
            yo = tn_rr[0]; tn_rr[0] = 1 - yo
            for hf in range(2):
                bks = grp[hf]
                t_g = sg_rr[0]; sg_rr[0] = 1 - t_g
                ACT(lambda e, t_g=t_g, b=bks[0]: e.activation(out=sg5[t_g][0:R, :], in_=banks[b][0:R, :], func=AF.Sigmoid), reads=[banks_b[bks[0]]], writes=[sg5_b[t_g]])
                DVE(lambda e, t_g=t_g, b=bks[1]: e.tensor_tensor(out=sg5[t_g][0:R, :], in0=banks[b][0:R, :], in1=sg5[t_g][0:R, :], op=ALU.mult),
                    reads=[banks_b[bks[1]], sg5_b[t_g]], writes=[sg5_b[t_g]])
                PO(c)(lambda e, s=s, hf=hf, t_g=t_g, yo=yo: e.tensor_tensor(out=tns[yo][0:R, hf * 512:(hf + 1) * 512], in0=c.xres(s)[:, hf * 512:(hf + 1) * 512],
                                                                           in1=sg5[t_g][0:R, :], op=ALU.add),
                      reads=[c.xres_b[s], sg5_b[t_g]], writes=[tns_b[yo]])
            DMA(DQ(c), f"y_out_{yo}", lambda e, s=s, yo=yo: e.dma_start(out=c.y_dst[s * R:(s + 1) * R, :], in_=tns[yo][0:R, :]), reads=[tns_b[yo]], is_output=True)
            if post_b is not None:
                post_b(s)

    preloaded = set()

    def prologue_sub(c, s):
        if (c.key, s) not in preloaded:
            s_load_sub(c, s)
        s_norm_T_sub(c, s, 0)

    def stage_loads(n):
        for s in range(n.NSUB):
            DMA(DQ(n), f"x_in_{n.xkey(s)}", lambda e, s=s: e.dma_start(out=xst(s), in_=n.x_src[s * 128:(s + 1) * 128, :]), writes=xst_b(s))

    def prologue_staged_a(n, s):
        xi = norm_to_bf16(n, s, 0, src=xst(s), src_b=xst_b(s))
        transposes_sub(n, s, xb[xi][0:n.R, :], [xb_b[xi]], 8, n.actT, n.actT_sub(s), evac_dve=True)

    def prologue_staged_b(n, s):
        DVE(lambda e, s=s: e.tensor_copy(out=n.xres(s), in_=xst(s)), reads=xst_b(s), writes=[n.xres_b[s]])

    def emit_group(ctxs, nxt):
        stage[0] = "0"
        for c in ctxs:
            if c.kind == "sample" or c.first:
                pend_ev = []
                for s in range(c.NSUB):
                    if (c.key, s) not in preloaded:
                        s_load_sub(c, s)
                    evs = s_norm_T_sub(c, s, 0, defer_evac=True)
                    for ev in pend_ev:
                        ev()
                    pend_ev = evs
                for ev in pend_ev:
                    ev()
                if c.first:
                    emit_vecT()
        gj_h, s_h = acquire("H")
        wh = sview(slots[s_h], 8, D)
        for jb in range(2):
            gj, si = acquire(f"BC{jb}")
            if jb == 0:
                gj_u, s_uu = acquire("U")
                wvu = sview(slots[s_uu], 8, D)
            wbc = sview(slots[si], 8, 1024)
            for half in range(2):
                j = 2 * jb + half
                stage[0] = f"1A{j}"
                for c in ctxs:
                    if j == 0:
                        s_pre(c, "1A")
                    s_1A(c, jb, (2 * half, 2 * half + 1), wbc, wh, [slots_b[si], slots_b[s_h]])
                if half == 1:
                    release(gj)
                for c in ctxs:
                    if j == 0:
                        s_pre(c, "1P")
                    s_1P(c, wvu, slots_b[s_uu], chs=range(2 * j, 2 * j + 2))
        release(gj_h)
        release(gj_u)
        stage[0] = "1G"
        gj_g, s_g = acquire("G0")
        gj_ap, s_ap = acquire("AO")
        gj_pw, s_pw = acquire("PW")
        wa = sview(slots[s_ap], 8, D)
        wp = sview(slots[s_pw], 8, 256)
        for jg in range(2):
            if jg == 1:
                release(gj_g)
                gj_g, s_g = acquire("G1")
            wg = sview(slots[s_g], 8, 1024)
            for c in ctxs:
                s_1G(c, jg, wg, wa, wp, [slots_b[s_g], slots_b[s_ap], slots_b[s_pw]])
        release(gj_g)
        release(gj_ap)
        release(gj_pw)
        stage[0] = "2"
        gj, s_o = acquire("O")
        wo = sview(slots[s_o], 8, D)
        for c in ctxs:
            s_tm(c, c.mg, c.bufs(16, 24), 8, lambda k, hf: wo[:, k, hf * 512:(hf + 1) * 512], [slots_b[s_o]], 1,
                 post=lambda s, c=c: s_norm_T_sub(c, s, 2, defer_evac=True), lag=2)
        release(gj)
        for c in ctxs:
            s_pre(c, "3")
        for j in range(6):
            nch = 4 if j < 5 else 2
            stage[0] = f"3_{j}"
            gj, s_u = acquire(f"UP{j}")
            wu = sview(slots[s_u], 8, 1024)
            for c in ctxs:
                s_3(c, j, nch, wu, slots_b[s_u])
            release(gj)
        for c in ctxs:
            s_3_flush(c)
        stage[0] = "4"
        gd = [acquire(f"D{kb}") for kb in range(3)]
        wd = [sview(slots[gd[kb][1]], DKB[kb][1], D) for kb in range(3)]
        for c in ctxs:
            s_tm(c, c.ff, c.bufs(0, NFC), NFC, lambda k, hf: wd[k // 8][:, k % 8, hf * 512:(hf + 1) * 512], [slots_b[g_[1]] for g_ in gd], 3,
                 post=lambda s, c=c: s_5pre_sub(c, s), lag=1)
        for g_ in gd:
            release(g_[0])
        stage[0] = "5"
        gj_e, s_e = acquire("E1")
        gj_e2, s_e2 = acquire("E2")
        wgt = sview(slots[s_e], 8, D)
        wpr = sview(slots[s_e2], 2, D)
        for c in ctxs:
            if c.kind == "prompt" and nxt is not None:
                stage_loads(nxt)
                s_5(c, wgt, wpr, [slots_b[s_e], slots_b[s_e2]], post_a=lambda s: prologue_staged_a(nxt, s), post_b=lambda s: prologue_staged_b(nxt, s))
            else:
                s_5(c, wgt, wpr, [slots_b[s_e], slots_b[s_e2]])
        release(gj_e)
        release(gj_e2)

    def emit_state_out(carr, carr_b, nch, nrows, dst):
        rts = [(0, min(128, nrows))] + ([(128, nrows - 128)] if nrows > 128 else [])
        for (r0, rn) in rts:
            for g0 in range(0, nch, 2):
                gn = min(2, nch - g0)
                bk = take_banks(1)[0]

                def fn(e, g0=g0, gn=gn, bk=bk, r0=r0, rn=rn):
                    ins = None
                    for i in range(gn):
                        ins = e.transpose(out=banks[bk][0:rn, i * 128:(i + 1) * 128], in_=carr[:, g0 + i, r0:r0 + rn], identity=identf[:])
                    return ins
                PE(fn, reads=[carr_b[g0 + i] for i in range(gn)] + [ident_b], writes=[banks_b[bk]])
                ti = take_tmp()
                DVE(lambda e, bk=bk, ti=ti, gn=gn, rn=rn: e.tensor_copy(out=temps[ti][0:rn, 0:gn * 128], in_=banks[bk][0:rn, 0:gn * 128]), reads=[banks_b[bk]], writes=[temps_b[ti]])
                DMA("pool", "st_out", lambda e, ti=ti, g0=g0, gn=gn, r0=r0, rn=rn: e.dma_start(out=dst[r0:r0 + rn, g0 * 128:(g0 + gn) * 128], in_=temps[ti][0:rn, 0:gn * 128]),
                    reads=[temps_b[ti]], is_output=True)

    pctx = [make_prompt_ctx(t) for t in range(nt_prompt)]
    sctx = make_sample_ctx() if with_sample else None
    for c0 in [pctx[0]] + ([sctx] if with_sample else []):
        for s in range(c0.NSUB):
            s_load_sub(c0, s)
            preloaded.add((c0.key, s))
    if with_sample:
        pre_state[("sca", 0, 0)] = state_dma(sca, "sca", 0, SPC * 2, 0, D)
        pre_state[("spl", 0, 0)] = state_dma(spl, "spl", 0, 128, 0, D)
    pump_loads()
    for t in range(nt_prompt):
        ctxs = [pctx[t]]
        if t == 0 and with_sample:
            ctxs.append(sctx)
        emit_group(ctxs, pctx[t + 1] if t + 1 < nt_prompt else None)
        if t == 0 and with_sample:
            emit_state_out(cA_s, cA_sb, 8, SPC * 2, ncs)
            emit_state_out(cP_s, cP_sb, 8, SPC * 15, nps)
            emit_state_out(cF_s, cF_sb, NFC, SPC * 2, nfs)
    emit_state_out(cA_p, cA_pb, 8, 2, ncp)
    emit_state_out(cP_p, cP_pb, 8, 15, npp)
    emit_state_out(cF_p, cF_pb, NFC, 2, nfp)

    eng_names = {"pe": "tensor", "act": "scalar", "dve": "vector", "pool": "gpsimd", "sp": "sync"}
    sems = {}
    for key in list(T.ENG) + list(T.dma_count.keys()):
        sems[key] = es.enter_context(nc.semaphore(f"s_{key}"))
    final = {}
    for tok in T.out_dma:
        final[tok[0]] = max(final.get(tok[0], 0), tok[1])

    block = es.enter_context(nc.Block())

    class _Cnt:
        def __init__(self, e):
            self._e = e; self.n = 0

        def matmul(self, *a, **k):
            self.n += 1
            return self._e.matmul(*a, **k)

        def transpose(self, *a, **k):
            self.n += 1
            return self._e.transpose(*a, **k)

    pe_counts = []

    def make_stream(ename):
        def body(e):
            for waits, fn, inc in T.streams[ename]:
                for key, val in waits:
                    e.wait_ge(sems[key], val)
                if ename == "pe":
                    ce = _Cnt(e)
                    ins = fn(ce)
                    pe_counts.append(ce.n)
                else:
                    ins = fn(e)
                ins.then_inc(sems[inc[0]], inc[1])
            if ename == "sp":
                for key, val in final.items():
                    e.wait_ge(sems[key], val)
        return body

    for ename in T.ENG:
        getattr(block, eng_names[ename])(make_stream(ename))
    es.close()
    nc._pe_info = list(zip(pe_labels, pe_counts))
    return nc


_CACHE = {}


def _get_nc(nt_prompt, with_sample):
    key = (nt_prompt, with_sample)
    if key not in _CACHE:
        _CACHE[key] = build_program(nt_prompt, with_sample)
    return _CACHE[key]


def make_in_maps(inputs, nt_prompt=4, ncores=NCORES):
    f = lambda a: np.ascontiguousarray(np.asarray(a, dtype=np.float32))
    ntok = nt_prompt * 512
    shared = {
        "g_pre_mix": f(inputs["g_pre_mix"][0]), "w_in": f(inputs["w_in"][0]), "conv_a_w": f(inputs["conv_a_w"][0]),
        "w_a_out": f(inputs["w_a_out"][0]), "pool_w": f(inputs["pool_w"][0]), "pool_scale": f(inputs["pool_scale"][0]),
        "w_o": f(inputs["w_o"][0]), "g_post_mix": f(inputs["g_post_mix"][0]), "g_pre_ffn": f(inputs["g_pre_ffn"][0]),
        "w_up": f(inputs["w_up"][0]), "ffn_conv_w": f(inputs["ffn_conv_w"][0]), "w_down": f(inputs["w_down"][0]),
        "g_post_ffn": f(inputs["g_post_ffn"][0]), "w_ple_proj": f(inputs["w_ple_proj"][0]), "w_ple_gate": f(inputs["w_ple_gate"][0]),
    }
    maps = []
    for b in range(ncores):
        m = dict(shared)
        m["xp"] = f(inputs["x_prompt"][b, :ntok])
        m["pp"] = f(inputs["p_prompt"][0, b, :ntok])
        sl = slice(b * SPC, (b + 1) * SPC)
        m["xs"] = f(inputs["x_sample"][sl]).reshape(SPC * DEC_S, D)
        m["ps"] = f(inputs["p_sample"][0, sl]).reshape(SPC * DEC_S, PLE)
        m["sca"] = f(inputs["state_conv_a"][0, sl]).reshape(SPC * 2, D)
        m["spl"] = f(inputs["state_pool"][0, sl]).reshape(SPC * 15, D)
        m["sff"] = f(inputs["state_ffn_conv"][0, sl]).reshape(SPC * 2, DFF)
        maps.append(m)
    return maps


def kernel(**inputs):
    nc = _get_nc(4, True)
    maps = make_in_maps(inputs)
    res = run_bass_kernel_spmd(nc, maps, core_ids=list(range(NCORES)))
    r = res.results
    y_prompt = np.stack([r[b]["yp"] for b in range(NCORES)], axis=0).astype(np.float32)
    y_sample = np.concatenate([r[b]["ys"].reshape(SPC, DEC_S, D) for b in range(NCORES)], axis=0).astype(np.float32)
    ncp = np.stack([r[b]["ncp"] for b in range(NCORES)], axis=0)[None].astype(np.float32)
    npp = np.stack([r[b]["npp"] for b in range(NCORES)], axis=0)[None].astype(np.float32)
    nfp = np.stack([r[b]["nfp"] for b in range(NCORES)], axis=0)[None].astype(np.float32)
    ncs = np.concatenate([r[b]["ncs"].reshape(SPC, 2, D) for b in range(NCORES)], axis=0)[None].astype(np.float32)
    nps = np.concatenate([r[b]["nps"].reshape(SPC, 15, D) for b in range(NCORES)], axis=0)[None].astype(np.float32)
    nfs = np.concatenate([r[b]["nfs"].reshape(SPC, 2, DFF) for b in range(NCORES)], axis=0)[None].astype(np.float32)
    return (y_prompt, y_sample, ncp, npp, nfp, ncs, nps, nfs)
```
